# Optimizing a Trainium2 kernel written in Bass

```python
import math
import jax, jax.numpy as jnp
from jax import lax
import numpy as np

D_MODEL = 2048
BATCH = 8
SEQ = 2048
DEPTH = 4

N_HEADS = 16
HEAD_DIM = D_MODEL // N_HEADS
D_FF = 256 * ((8 * D_MODEL // 3 + 255) // 256)
N_MIXERS = 3
Q_BLOCK = 128
REL_BUCKETS = 32
REL_MAX_DIST = 2048
DILATED_BRANCHES = ((128, 1), (512, 4), (2048, 16))
RMS_EPS = 1e-6
FFN_HALF = 0.5

kernel_name = "hybrid_fox_dilated_stickbreaking_macaron"


def rms_norm(x, g):
    xf = x.astype(jnp.float32)
    y = xf * lax.rsqrt(jnp.mean(xf * xf, axis=-1, keepdims=True) + RMS_EPS)
    return (y * g.astype(jnp.float32)).astype(x.dtype)


def swiglu(x, w_gate, w_up, w_down):
    return (jax.nn.silu(x @ w_gate) * (x @ w_up)) @ w_down


def split_heads(t):
    b, s, _ = t.shape
    return t.reshape(b, s, N_HEADS, HEAD_DIM).transpose(0, 2, 1, 3)


def merge_heads(o):
    b, h, s, dh = o.shape
    return o.transpose(0, 2, 1, 3).reshape(b, s, h * dh)


def to_query_blocks(t):
    b, h, s = t.shape[:3]
    t = t.reshape((b, h, s // Q_BLOCK, Q_BLOCK) + t.shape[3:])
    return jnp.moveaxis(t, 2, 0)


def from_query_blocks(o):
    nb, b, h, q, dh = o.shape
    return jnp.moveaxis(o, 0, 2).reshape(b, h, nb * q, dh)


def t5_bucket(dist):
    max_exact = REL_BUCKETS // 2
    d = jnp.maximum(dist, 1).astype(jnp.float32)
    large = max_exact + (jnp.log(d / max_exact) / math.log(REL_MAX_DIST / max_exact)
                         * (REL_BUCKETS - max_exact)).astype(jnp.int32)
    large = jnp.minimum(large, REL_BUCKETS - 1)
    return jnp.where(dist < max_exact, dist, large)


def forgetting_attention(h, w_in, b_f):
    proj = h @ w_in
    q, k, v = [split_heads(t) for t in jnp.split(proj[..., :3 * D_MODEL], 3, axis=-1)]
    log_f = jax.nn.log_sigmoid((proj[..., 3 * D_MODEL:] + b_f).astype(jnp.float32))
    c = jnp.cumsum(log_f, axis=1).transpose(0, 2, 1)
    seq = h.shape[1]
    nb = seq // Q_BLOCK
    scale = HEAD_DIM ** -0.5
    kpos = jnp.arange(seq)

    def block(args):
        qb, cb, n = args
        qpos = n * Q_BLOCK + jnp.arange(Q_BLOCK)
        s = jnp.einsum('bhqd,bhkd->bhqk', qb, k).astype(jnp.float32) * scale
        s = s + cb[..., :, None] - c[..., None, :]
        s = jnp.where(kpos[None, :] <= qpos[:, None], s, -jnp.inf)
        p = jax.nn.softmax(s, axis=-1).astype(v.dtype)
        return jnp.einsum('bhqk,bhkd->bhqd', p, v)

    o = lax.map(block, (to_query_blocks(q), to_query_blocks(c), jnp.arange(nb)))
    return merge_heads(from_query_blocks(o))


def dilated_branch(q, k, v, rel_bias, window, dilation):
    b, h, seq, dh = q.shape
    span = window // dilation
    length = seq // dilation
    padded = -(-length // Q_BLOCK) * Q_BLOCK
    nb = padded // Q_BLOCK

    def strided(t):
        t = t.reshape(b, h, length, dilation, dh).transpose(0, 1, 3, 2, 4)
        t = jnp.pad(t, ((0, 0), (0, 0), (0, 0), (0, padded - length), (0, 0)))
        return t.reshape(b, h, dilation, nb, Q_BLOCK, dh)

    def with_prev(t):
        prev = jnp.pad(t, ((0, 0), (0, 0), (0, 0), (1, 0), (0, 0), (0, 0)))[:, :, :, :-1]
        return jnp.concatenate([prev, t], axis=-2)

    qs = strided(q)
    kb = with_prev(strided(k))
    vb = with_prev(strided(v))
    qi = jnp.arange(Q_BLOCK)[:, None]
    kj = jnp.arange(2 * Q_BLOCK)[None, :]
    off = Q_BLOCK + qi - kj
    band = (off >= 0) & (off <= span)
    key_idx = jnp.arange(nb)[:, None, None] * Q_BLOCK - Q_BLOCK + kj[None]
    valid = band[None] & (key_idx >= 0)
    bias = rel_bias[:, t5_bucket(jnp.clip(off, 0) * dilation)].astype(jnp.float32)
    s = jnp.einsum('bhrnqd,bhrnkd->bhrnqk', qs, kb).astype(jnp.float32) * (HEAD_DIM ** -0.5)
    s = jnp.where(valid, s + bias[:, None, None], -jnp.inf)
    m = jnp.max(s, axis=-1, keepdims=True)
    e = jnp.exp(s - m)
    den = jnp.sum(e, axis=-1, keepdims=True)
    lse = (m + jnp.log(den))[..., 0]
    o = jnp.einsum('bhrnqk,bhrnkd->bhrnqd', (e / den).astype(v.dtype), vb)

    def unstride(t):
        t = t.reshape((b, h, dilation, padded) + t.shape[5:])[:, :, :, :length]
        t = jnp.swapaxes(t, 2, 3)
        return t.reshape((b, h, seq) + t.shape[4:])

    return unstride(o), unstride(lse)


def dilated_attention(h, w_in, rel_bias):
    proj = h @ w_in
    q, k, v = [split_heads(t) for t in jnp.split(proj, 3, axis=-1)]
    outs, lses = [], []
    for window, dilation in DILATED_BRANCHES:
        o, lse = dilated_branch(q, k, v, rel_bias, window, dilation)
        outs.append(o)
        lses.append(lse)
    weights = jax.nn.softmax(jnp.stack(lses, axis=0), axis=0)
    o = sum(weights[g][..., None].astype(v.dtype) * outs[g] for g in range(len(outs)))
    return merge_heads(o)


def stick_breaking_attention(h, w_in):
    proj = h @ w_in
    q, k, v = [split_heads(t) for t in jnp.split(proj, 3, axis=-1)]
    seq = h.shape[1]
    nb = seq // Q_BLOCK
    scale = HEAD_DIM ** -0.5
    kpos = jnp.arange(seq)

    def block(args):
        qb, n = args
        qpos = n * Q_BLOCK + jnp.arange(Q_BLOCK)
        z = jnp.einsum('bhqd,bhkd->bhqk', qb, k).astype(jnp.float32) * scale
        causal = kpos[None, :] < qpos[:, None]
        log_beta = jax.nn.log_sigmoid(z)
        log_one_minus = jnp.where(causal, jax.nn.log_sigmoid(-z), 0.0)
        after = lax.cumsum(log_one_minus, axis=3, reverse=True) - log_one_minus
        a = jnp.where(causal, jnp.exp(log_beta + after), 0.0)
        return jnp.einsum('bhqk,bhkd->bhqd', a.astype(v.dtype), v)

    o = lax.map(block, (to_query_blocks(q), jnp.arange(nb)))
    return merge_heads(from_query_blocks(o))


def setup_inputs(seed: int = 0) -> dict:
    key = jax.random.key(seed)
    ks = jax.random.split(key, 14)
    f32 = jnp.float32
    D, F, H = D_MODEL, D_FF, N_HEADS

    def dense(k, shape, fan_in):
        return jax.random.normal(k, shape, f32) * fan_in ** -0.5

    return {
        "x": jax.random.normal(ks[0], (BATCH, SEQ, D), f32),
        "norm_g": 1.0 + 0.05 * jax.random.normal(ks[1], (DEPTH, 3, D), f32),
        "ffn_w_gate": dense(ks[2], (DEPTH, 2, D, F), D),
        "ffn_w_up": dense(ks[3], (DEPTH, 2, D, F), D),
        "ffn_w_down": dense(ks[4], (DEPTH, 2, F, D), F),
        "w_in_0": dense(ks[5], (D, 3 * D + H), D),
        "b_f_0": 2.0 + 0.1 * jax.random.normal(ks[6], (H,), f32),
        "w_in_1": dense(ks[7], (D, 3 * D), D),
        "w_in_2": dense(ks[8], (D, 3 * D), D),
        "w_in_3": dense(ks[9], (D, 3 * D + H), D),
        "b_f_3": 2.0 + 0.1 * jax.random.normal(ks[10], (H,), f32),
        "w_out": dense(ks[11], (DEPTH, D, D), D),
        "rel_bias": 0.5 * jax.random.normal(ks[12], (H, REL_BUCKETS), f32),
        "final_g": 1.0 + 0.05 * jax.random.normal(ks[13], (D,), f32),
    }


def reference(x, norm_g, ffn_w_gate, ffn_w_up, ffn_w_down, w_in_0, b_f_0, w_in_1, w_in_2,
              w_in_3, b_f_3, w_out, rel_bias, final_g):
    w_ins = (w_in_0, w_in_1, w_in_2, w_in_3)
    forget_biases = {0: b_f_0, 3: b_f_3}
    for i in range(DEPTH):
        x = x + FFN_HALF * swiglu(rms_norm(x, norm_g[i, 0]),
                                  ffn_w_gate[i, 0], ffn_w_up[i, 0], ffn_w_down[i, 0])
        h = rms_norm(x, norm_g[i, 1])
        mixer = i % N_MIXERS
        if mixer == 0:
            m = forgetting_attention(h, w_ins[i], forget_biases[i])
        elif mixer == 1:
            m = dilated_attention(h, w_ins[i], rel_bias)
        else:
            m = stick_breaking_attention(h, w_ins[i])
        x = x + m @ w_out[i]
        x = x + FFN_HALF * swiglu(rms_norm(x, norm_g[i, 2]),
                                  ffn_w_gate[i, 1], ffn_w_up[i, 1], ffn_w_down[i, 1])
    return rms_norm(x, final_g)
```

```python
import contextlib
import math
import numpy as np
import concourse.bass as bass
import concourse.mybir as mybir
from concourse.bass_utils import run_bass_kernel_spmd

F32 = mybir.dt.float32
BF16 = mybir.dt.bfloat16
AF = mybir.ActivationFunctionType
ALU = mybir.AluOpType

S = 2048
D = 2048
FF = 5632
H = 16
DH = 128
DEPTH = 4
NCH = 16
NFC = 44
EPS = 1e-6
NEG = -30000.0
NCORES = 8
NF = 2176

W_NAMES = ["norm_g", "ffn_w_gate", "ffn_w_up", "ffn_w_down", "w_in_0", "b_f_0", "w_in_1",
           "w_in_2", "w_in_3", "b_f_3", "w_out", "rel_bias", "final_g"]
W_SHAPES = {
    "norm_g": [DEPTH, 3, D], "ffn_w_gate": [DEPTH, 2, D, FF], "ffn_w_up": [DEPTH, 2, D, FF],
    "ffn_w_down": [DEPTH, 2, FF, D], "w_in_0": [D, 3 * D + H], "b_f_0": [H],
    "w_in_1": [D, 3 * D], "w_in_2": [D, 3 * D], "w_in_3": [D, 3 * D + H], "b_f_3": [H],
    "w_out": [DEPTH, D, D], "rel_bias": [H, 32], "final_g": [D],
}


def _t5_bucket_np(dist):
    dist = np.asarray(dist, dtype=np.int64)
    d = np.maximum(dist, 1).astype(np.float32)
    large = 16 + (np.log(d / np.float32(16.0)) / np.float32(math.log(2048 / 16)) * np.float32(16.0)).astype(np.int32)
    large = np.minimum(large, 31)
    return np.where(dist < 16, dist, large).astype(np.int64)


def make_consts():
    i = np.arange(128)
    c128 = np.zeros((7, 128, 128), np.float32)
    c128[0] = np.eye(128)
    c128[1] = (i[:, None] <= i[None, :])
    c128[2] = (i[:, None] > i[None, :])
    c128[3] = (i[:, None] + i[None, :] == 127)
    c128[4] = np.where(i[:, None] <= i[None, :], 0.0, NEG)
    c128[5] = (i[:, None] < i[None, :])
    c128[6] = np.where(i[:, None] < i[None, :], 0.0, NEG)
    c128 = np.ascontiguousarray(c128.transpose(1, 0, 2).reshape(128, 7 * 128))
    sel = np.zeros((16, 16 * 128), np.float32)
    for h in range(16):
        sel[h, h * 128:(h + 1) * 128] = 1.0
    ohlm = np.zeros((33, NF), np.float32)
    for idx in range(NF):
        dl = idx - 127
        if dl < 0 or dl > 2048:
            ohlm[32, idx] = NEG
            continue
        mult = (1 if dl <= 128 else 0) + (1 if (dl % 4 == 0 and dl <= 512) else 0) + \
               (1 if (dl % 16 == 0 and dl <= 2048) else 0)
        if mult == 0:
            ohlm[32, idx] = NEG
        else:
            ohlm[32, idx] = math.log(mult)
            ohlm[int(_t5_bucket_np(dl)), idx] = 1.0
    return {"c128": c128, "sel": sel, "ohlm": ohlm}


class Buf:
    __slots__ = ("name", "w", "r")

    def __init__(self, name):
        self.name = name
        self.w = None
        self.r = {}


class Sched:
    ENG = ("pe", "act", "dve", "pool", "sp")

    def __init__(self, nc, es):
        self.nc = nc
        self.es = es
        self.e = {"pe": nc.tensor, "act": nc.scalar, "dve": nc.vector, "pool": nc.gpsimd, "sp": nc.sync}
        self.sem = {k: es.enter_context(nc.semaphore("sem_" + k)) for k in self.ENG}
        self.cnt = {k: 0 for k in self.ENG}
        self.seen = {k: {} for k in self.ENG}
        self.dsems = []
        self.same_sync = True
        self.nwait = 0
        self.nops = 0

    def dma_sem(self, name):
        s = self.es.enter_context(self.nc.semaphore(name))
        box = [s, 0]
        self.dsems.append(box)
        return box

    def _wait(self, eng, sem, val):
        k = id(sem)
        if self.seen[eng].get(k, 0) < val:
            self.e[eng].wait_ge(sem, val)
            self.seen[eng][k] = val
            self.nwait += 1

    def _need(self, eng, reads, writes):
        need = {}

        def add(tok):
            s, v, owner = tok
            if owner == eng and (eng in ("pe", "sp") or not self.same_sync):
                return
            k = id(s)
            if k not in need or need[k][1] < v:
                need[k] = (s, v)

        for b in reads:
            if b.w is not None:
                add(b.w)
        for b in writes:
            if b.w is not None:
                add(b.w)
            for t in b.r.values():
                add(t)
        for s, v in need.values():
            self._wait(eng, s, v)

    def _commit(self, tok, reads, writes):
        for b in writes:
            b.w = tok
            b.r = {}
        k = id(tok[0])
        for b in reads:
            if b in writes:
                continue
            o = b.r.get(k)
            if o is None or o[1] < tok[1]:
                b.r[k] = tok

    def op(self, eng, fn, reads=(), writes=()):
        self._need(eng, reads, writes)
        inst = fn()
        self.cnt[eng] += 1
        inst.then_inc(self.sem[eng], 1)
        self.nops += 1
        self._commit((self.sem[eng], self.cnt[eng], eng), reads, writes)

    def dma(self, q, pairs, reads, writes, box, **kw):
        self._need(q, reads, writes)
        for (o, i) in pairs:
            inst = self.e[q].dma_start(out=o, in_=i, **kw)
            box[1] += 16
            inst.then_inc(box[0], 16)
        self._commit((box[0], box[1], None), reads, writes)

    def barrier(self, bufs_to_reset=()):
        for eng in self.ENG:
            for k in self.ENG:
                if self.cnt[k] > 0 and not (k == eng and eng in ("pe", "sp")):
                    self._wait(eng, self.sem[k], self.cnt[k])
            for box in self.dsems:
                if box[1] > 0:
                    self._wait(eng, box[0], box[1])
        for b in bufs_to_reset:
            b.w = None
            b.r = {}


class Prog:
    def __init__(self, layers=(0, 1, 2, 3), subs=("ffn1", "attn", "ffn2"), do_in=True, do_out=True,
                 compact=False):
        self.compact = compact
        self.layers = list(layers)
        self.subs = tuple(subs)
        self.do_in = do_in
        self.do_out = do_out
        self.nc = bass.Bass("TRN2", target_bir_lowering=False)
        self.build()

    def build(self):
        nc = self.nc
        with contextlib.ExitStack() as es:
            self.es = es
            s = self.s = Sched(nc, es)
            self.x = nc.dram_tensor("x", [S, D], F32, kind="ExternalInput").ap()
            self.dram_in = {}
            self.c128_d = nc.dram_tensor("c128", [128, 7 * 128], F32, kind="ExternalInput").ap()
            self.sel_d = nc.dram_tensor("sel", [16, 2048], F32, kind="ExternalInput").ap()
            self.ohlm_d = nc.dram_tensor("ohlm", [33, NF], F32, kind="ExternalInput").ap()
            self.y = nc.dram_tensor("y", [S, D], F32, kind="ExternalOutput").ap()
            self.xT = nc.dram_tensor("xT_scr", [D, S], F32, kind="Internal").ap()
            self.mTd = nc.dram_tensor("mT_scr", [D, S], BF16, kind="Internal").ap()
            self.fpd = nc.dram_tensor("fp_scr", [16, NF], F32, kind="Internal").ap()
            self.xTb = [[Buf(f"xT{c}_{q}") for q in range(4)] for c in range(NCH)]
            self.mTdb = [[Buf(f"mTd{c}_{q}") for q in range(4)] for c in range(NCH)]
            self.fpdb = Buf("fpd")

            def sb(name, shape, dt):
                return es.enter_context(nc.sbuf_tensor(self.nm() + name, shape, dt))

            self.ps = [es.enter_context(nc.psum_tensor(f"ps{i}", [128, 512], F32)) for i in range(8)]
            self.psb = [Buf(f"ps{i}") for i in range(8)]
            self.c128 = sb("c128_sb", [128, 7 * 128], F32)
            self.ones_bf = sb("ones_bf", [128, 128], BF16)
            self.ones_f = sb("ones_f", [128, 128], F32)
            self.gcol = sb("gcol", [128, 208], F32)
            self.eps_t = sb("eps_t", [128, 1], F32)
            self.rstd = sb("rstd", [128, S], F32)
            self.rstdq = [Buf(f"rstd{q}") for q in range(4)]
            self.have_stats = False
            self.pend_stats = None
            self.sq = [sb(f"sq{i}", [128, 1024], BF16) for i in range(2)]
            self.sqb = [Buf(f"sq{i}") for i in range(2)]
            self.sqh = [Buf(f"sqh{i}") for i in range(4)]
            self.xin = [sb(f"xin{i}", [128, 1024], F32) for i in range(3)]
            self.xinb = [Buf(f"xin{i}") for i in range(3)]
            self.xin_sem = [s.dma_sem(f"xin_sem{i}") for i in range(3)]
            self.xin_i = 0
            self.NSLOT = 2
            self.wsl = [sb(f"wsl{i}", [128, 8192], BF16) for i in range(self.NSLOT)]
            self.wslb = [Buf(f"wsl{i}") for i in range(self.NSLOT)]
            self.wsl_sem = [s.dma_sem(f"wsl_sem{i}") for i in range(self.NSLOT)]
            self.misc_sem = s.dma_sem("misc_sem")
            self.cb = Buf("consts")

            self.jobs = list(self.plan_jobs())
            self.job_issued = 0
            self.job_next = 0

            self.setup_consts()
            if self.do_in:
                self.phase_in()
            for l in self.layers:
                if "ffn1" in self.subs:
                    self.phase_ffn(l, 0)
                if "attn" in self.subs:
                    self.phase_attn(l)
                if "ffn2" in self.subs:
                    self.phase_ffn(l, 1)
            if self.do_out:
                self.phase_out()
            assert self.job_next == len(self.jobs), (self.job_next, len(self.jobs))
            s.barrier()

    def nm(self):
        self._uid = getattr(self, "_uid", 0) + 1
        return f"u{self._uid}_"

    def din(self, name, shape):
        if name not in self.dram_in:
            self.dram_in[name] = self.nc.dram_tensor(name, list(shape), F32, kind="ExternalInput").ap()
        return self.dram_in[name]

    def W_ff(self, nm, l, j):
        shp = W_SHAPES[nm]
        if self.compact:
            return self.din(f"{nm}@{l}@{j}", shp[2:])
        return self.din(nm, shp)[l, j]

    def W_out(self, l):
        if self.compact:
            return self.din(f"w_out@{l}", [D, D])
        return self.din("w_out", W_SHAPES["w_out"])[l]

    def W_in(self, l):
        return self.din(f"w_in_{l}", W_SHAPES[f"w_in_{l}"])

    def W_small(self, nm):
        return self.din(nm, W_SHAPES[nm])

    def ident(self):
        return self.c128[:, 0:128]

    def cmat(self, i):
        return self.c128[:, i * 128:(i + 1) * 128]

    def setup_consts(self):
        s, nc = self.s, self.nc
        s.dma("sp", [(self.c128[:], self.c128_d)], [], [self.cb], self.misc_sem)
        s.op("dve", lambda: nc.vector.memset(self.ones_bf[:], 1.0), [], [self.cb])
        s.op("dve", lambda: nc.vector.memset(self.ones_f[:], 1.0), [], [self.cb])
        s.op("dve", lambda: nc.vector.memset(self.eps_t[:], EPS), [], [self.cb])
        with contextlib.ExitStack() as ph:
            ga = ph.enter_context(nc.sbuf_tensor(self.nm() + "ga", [128, 128], F32))
            gb = ph.enter_context(nc.sbuf_tensor(self.nm() + "gb", [80, 128], F32))
            gbuf = Buf("gab")
            ng = self.W_small("norm_g").rearrange("l j (c p) -> (l j c) p", p=128)
            fg = self.W_small("final_g").rearrange("(c p) -> c p", p=128)
            s.dma("sp", [(ga[:], ng[0:128, :]), (gb[0:64, :], ng[128:192, :]), (gb[64:80, :], fg)],
                  [], [gbuf], self.misc_sem)
            p0 = self.ps[0]
            s.op("pe", lambda: nc.tensor.transpose(out=p0[:, 0:128], in_=ga[:], identity=self.ident()),
                 [gbuf, self.cb], [self.psb[0]])
            s.op("pe", lambda: nc.tensor.transpose(out=p0[:, 128:208], in_=gb[0:80, :],
                                                   identity=self.c128[0:80, 0:80]),
                 [gbuf, self.cb], [self.psb[0]])
            s.op("dve", lambda: nc.vector.tensor_copy(out=self.gcol[:], in_=p0[:, 0:208]),
                 [self.psb[0]], [self.cb])
            s.barrier()

    def plan_jobs(self):
        for l in self.layers:
            if "ffn1" in self.subs:
                yield from self.ffn_jobs(l, 0)
            if "attn" in self.subs:
                for h in range(H):
                    yield ("wqkv", l, h)
                for og in range(4):
                    yield ("wo", l, og)
            if "ffn2" in self.subs:
                yield from self.ffn_jobs(l, 1)

    def ffn_jobs(self, l, j):
        for tt in range(2):
            for fg in range(22):
                yield ("wgu", l, j, fg)
            for oc in range(16):
                yield ("wd", l, j, oc)

    def issue_job(self, idx):
        job = self.jobs[idx]
        si = idx % self.NSLOT
        slot = self.wsl[si]
        kind = job[0]
        pairs = []
        if kind == "wgu":
            _, l, j, fg = job
            dst = slot[:, 0:8192].rearrange("p (m c f) -> p m c f", m=2, c=16)
            for m, nm in enumerate(("ffn_w_gate", "ffn_w_up")):
                src = self.W_ff(nm, l, j).rearrange("(c p) f -> p c f", p=128)[:, :, fg * 256:(fg + 1) * 256]
                pairs.append((dst[:, m], src))
        elif kind == "wd":
            _, l, j, oc = job
            dst = slot[:, 0:NFC * 128].rearrange("p (c o) -> p c o", c=NFC)
            src = self.W_ff("ffn_w_down", l, j).rearrange("(c p) o -> p c o", p=128)[:, :, oc * 128:(oc + 1) * 128]
            pairs.append((dst, src))
        elif kind == "wqkv":
            _, l, h = job
            dst = slot[:, 0:3 * 16 * 128].rearrange("p (m c n) -> p m c n", m=3, c=16)
            win = self.W_in(l).rearrange("(c p) n -> p c n", p=128)
            for m in range(3):
                pairs.append((dst[:, m], win[:, :, m * D + h * 128: m * D + (h + 1) * 128]))
        elif kind == "wo":
            _, l, og = job
            dst = slot[:, 0:8192].rearrange("p (c o) -> p c o", c=16)
            src = self.W_out(l).rearrange("(c p) o -> p c o", p=128)[:, :, og * 512:(og + 1) * 512]
            pairs.append((dst, src))
        self.s.dma("pool", pairs, [], [self.wslb[si]], self.wsl_sem[si])

    def acquire(self, kind):
        idx = self.job_next
        assert self.jobs[idx][0] == kind, (self.jobs[idx], kind)
        while self.job_issued < min(len(self.jobs), idx + self.NSLOT):
            self.issue_job(self.job_issued)
            self.job_issued += 1
        self.job_next += 1
        si = idx % self.NSLOT
        return self.wsl[si], self.wslb[si]

    def prefetch(self):
        idx = self.job_next
        while self.job_issued < min(len(self.jobs), idx + self.NSLOT - 1):
            self.issue_job(self.job_issued)
            self.job_issued += 1

    def xT_bufs(self, c, t0, T):
        return [self.xTb[c][q] for q in range(t0 // 512, (t0 + T) // 512)]

    def xin_load(self, c, t0, T):
        i = self.xin_i
        self.xin_i = (i + 1) % 3
        self.s.dma("sp", [(self.xin[i][:, 0:T], self.xT[c * 128:(c + 1) * 128, t0:t0 + T])],
                   self.xT_bufs(c, t0, T), [self.xinb[i]], self.xin_sem[i])
        return i

    def xin_store(self, i, c, t0, T):
        self.s.dma("sp", [(self.xT[c * 128:(c + 1) * 128, t0:t0 + T], self.xin[i][:, 0:T])],
                   [self.xinb[i]], self.xT_bufs(c, t0, T), self.xin_sem[i])

    def rstd_bufs(self, t0, T):
        return [self.rstdq[q] for q in range(t0 // 512, (t0 + T) // 512)]

    def finish_stats(self, bank, t0):
        s, nc = self.s, self.nc
        sl = self.rstd[:, t0:t0 + 512]
        rb_ = self.rstdq[t0 // 512]
        s.op("act", lambda: nc.scalar.activation(out=sl, in_=self.ps[bank][:, :], func=AF.Ln,
                                                 scale=1.0 / D, bias=self.eps_t[:, 0:1]),
             [self.psb[bank], self.cb], [rb_])
        s.op("act", lambda: nc.scalar.activation(out=sl, in_=sl, func=AF.Exp, scale=-0.5), [rb_], [rb_])

    def emit_stats(self, t0, T):
        s, nc = self.s, self.nc
        nh = T // 512
        for c in range(NCH):
            xi = self.xin_load(c, t0, T)
            qi = c % 2
            s.op("act", lambda: nc.scalar.activation(out=self.sq[qi][:, 0:T], in_=self.xin[xi][:, 0:T],
                                                     func=AF.Square),
                 [self.xinb[xi]], [self.sqb[qi]])

            def mm():
                r = None
                for hf in range(nh):
                    r = nc.tensor.matmul(self.ps[hf][:, :], lhsT=self.ones_bf[:],
                                         rhs=self.sq[qi][:, hf * 512:(hf + 1) * 512],
                                         start=(c == 0), stop=(c == NCH - 1))
                return r
            s.op("pe", mm, [self.sqb[qi], self.cb], [self.psb[hf] for hf in range(nh)])
        for hf in range(nh):
            self.finish_stats(hf, t0 + hf * 512)

    def next_stats(self, xi, col0, bank, t0, first, last):
        s, nc = self.s, self.nc
        self.sq_i = (getattr(self, "sq_i", 0) + 1) % 4
        qi, half = self.sq_i // 2, self.sq_i % 2
        sqv = self.sq[qi][:, half * 512:(half + 1) * 512]
        sqbuf = self.sqh[self.sq_i]
        s.op("act", lambda: nc.scalar.activation(out=sqv, in_=self.xin[xi][:, col0:col0 + 512], func=AF.Square),
             [self.xinb[xi]], [sqbuf, self.sqb[qi]])
        self.flush_stats()
        self.pend_stats = (sqv, sqbuf, bank, t0, first, last)

    def flush_stats(self):
        s, nc = self.s, self.nc
        if self.pend_stats is None:
            return
        sqv, sqbuf, bank, t0, first, last = self.pend_stats
        self.pend_stats = None
        s.op("pe", lambda: nc.tensor.matmul(self.ps[bank][:, :], lhsT=self.ones_bf[:], rhs=sqv,
                                            start=first, stop=last), [sqbuf, self.cb], [self.psb[bank]])
        if last:
            self.finish_stats(bank, t0)

    def emit_norm_chunk(self, t0, T, gi, c, dst, dbuf):
        s, nc = self.s, self.nc
        xi = self.xin_load(c, t0, T)
        s.op("dve", lambda: nc.vector.scalar_tensor_tensor(
            out=dst, in0=self.xin[xi][:, 0:T], scalar=self.gcol[:, gi * 16 + c: gi * 16 + c + 1],
            in1=self.rstd[:, t0:t0 + T], op0=ALU.mult, op1=ALU.mult),
            [self.xinb[xi], self.cb] + self.rstd_bufs(t0, T), [dbuf])

    def emit_norm(self, t0, T, gi, dst_fn, dst_bufs):
        s, nc = self.s, self.nc
        if not self.have_stats:
            self.emit_stats(t0, T)
        for c in range(NCH):
            xi = self.xin_load(c, t0, T)
            s.op("dve", lambda: nc.vector.scalar_tensor_tensor(
                out=dst_fn(c), in0=self.xin[xi][:, 0:T], scalar=self.gcol[:, gi * 16 + c: gi * 16 + c + 1],
                in1=self.rstd[:, t0:t0 + T], op0=ALU.mult, op1=ALU.mult),
                [self.xinb[xi], self.cb] + self.rstd_bufs(t0, T), [dst_bufs[c]])

    def phase_in(self):
        s, nc = self.s, self.nc
        with contextlib.ExitStack() as ph:
            xrow = [ph.enter_context(nc.sbuf_tensor(self.nm() + f"xrow{i}", [128, 2048], F32)) for i in range(2)]
            xcol = [ph.enter_context(nc.sbuf_tensor(self.nm() + f"xcol{i}", [128, 16, 128], F32)) for i in range(2)]
            xrb = [Buf("xrow0"), Buf("xrow1")]
            xcb = [Buf("xcol0"), Buf("xcol1")]
            sem_r = [s.dma_sem("xrow_sem0"), s.dma_sem("xrow_sem1")]
            sem_c = [s.dma_sem("xcol_sem0"), s.dma_sem("xcol_sem1")]
            xTv = self.xT.rearrange("(c p) t -> p c t", p=128)
            k = 0
            for tb in range(16):
                i = tb % 2
                s.dma("sp", [(xrow[i][:], self.x[tb * 128:(tb + 1) * 128, :])], [], [xrb[i]], sem_r[i])
                for c4 in range(4):
                    bank = 2 + (k % 2)
                    k += 1

                    def tr():
                        r = None
                        for u in range(4):
                            cc = c4 * 4 + u
                            r = nc.tensor.transpose(out=self.ps[bank][:, u * 128:(u + 1) * 128],
                                                    in_=xrow[i][:, cc * 128:(cc + 1) * 128],
                                                    identity=self.ident())
                        return r
                    s.op("pe", tr, [xrb[i], self.cb], [self.psb[bank]])
                    s.op("act", lambda: nc.scalar.copy(
                        out=xcol[i][:, c4 * 4:(c4 + 1) * 4, :],
                        in_=self.ps[bank][:, :].rearrange("p (u t) -> p u t", u=4)),
                        [self.psb[bank]], [xcb[i]])
                s.dma("sp", [(xTv[:, :, tb * 128:(tb + 1) * 128], xcol[i][:])], [xcb[i]],
                      [self.xTb[c][tb // 4] for c in range(NCH)], sem_c[i])
            s.barrier(xrb + xcb)

    def phase_out(self):
        s, nc = self.s, self.nc
        with contextlib.ExitStack() as ph:
            yn = ph.enter_context(nc.sbuf_tensor(self.nm() + "yn", [128, 16, 512], F32))
            ynb = [Buf(f"yn{c}") for c in range(NCH)]
            yrow = [ph.enter_context(nc.sbuf_tensor(self.nm() + f"yrow{i}", [128, 2048], F32)) for i in range(2)]
            yrb = [Buf("yrow0"), Buf("yrow1")]
            sem_y = [s.dma_sem("yrow_sem0"), s.dma_sem("yrow_sem1")]
            k = 0
            kk = 0
            for tq in range(4):
                self.emit_norm(tq * 512, 512, 12, lambda c: yn[:, c, :], ynb)
                for tb4 in range(4):
                    i = kk % 2
                    kk += 1
                    for c4 in range(4):
                        bank = 2 + (k % 2)
                        k += 1

                        def tr():
                            r = None
                            for u in range(4):
                                cc = c4 * 4 + u
                                r = nc.tensor.transpose(out=self.ps[bank][:, u * 128:(u + 1) * 128],
                                                        in_=yn[:, cc, tb4 * 128:(tb4 + 1) * 128],
                                                        identity=self.ident())
                            return r
                        s.op("pe", tr, [ynb[c4 * 4 + u] for u in range(4)] + [self.cb], [self.psb[bank]])
                        s.op("act", lambda: nc.scalar.copy(out=yrow[i][:, c4 * 512:(c4 + 1) * 512],
                                                           in_=self.ps[bank][:, :]),
                             [self.psb[bank]], [yrb[i]])
                    r0 = tq * 512 + tb4 * 128
                    s.dma("sp", [(self.y[r0:r0 + 128, :], yrow[i][:])], [yrb[i]], [], sem_y[i])
            s.barrier(ynb + yrb)

    def phase_ffn(self, l, j):
        s, nc = self.s, self.nc
        gi = l * 3 + (0 if j == 0 else 2)
        T = 1024
        with contextlib.ExitStack() as ph:
            hT = ph.enter_context(nc.sbuf_tensor(self.nm() + "hT_f", [128, 16, T], BF16))
            hTb = [Buf(f"hTf{c}") for c in range(NCH)]
            act = ph.enter_context(nc.sbuf_tensor(self.nm() + "act_f", [128, NFC, T], BF16))
            actb = [Buf(f"act{c}") for c in range(NFC)]
            sg = [ph.enter_context(nc.sbuf_tensor(self.nm() + f"sg{i}", [128, 512], F32)) for i in range(2)]
            sgb = [Buf("sg0"), Buf("sg1")]
            if not self.have_stats:
                self.emit_stats(0, T)
                self.emit_stats(T, T)
                self.have_stats = True
            self.emit_norm(0, T, gi, lambda c: hT[:, c, :], hTb)
            for tt in range(2):
                t0 = tt * T
                k = 0
                for fg in range(22):
                    slot, slb = self.acquire("wgu")
                    wv = slot[:, 0:8192].rearrange("p (m c f) -> p m c f", m=2, c=16)
                    for fcl in range(2):
                        fc = fg * 2 + fcl
                        for hf in range(2):
                            bg = 2 + (k % 2)
                            bu = 4 + (k % 2)
                            si = k % 2
                            k += 1

                            def mm(m, bank):
                                def f():
                                    r = None
                                    for c in range(NCH):
                                        r = nc.tensor.matmul(self.ps[bank][:, :],
                                                             lhsT=wv[:, m, c, fcl * 128:(fcl + 1) * 128],
                                                             rhs=hT[:, c, hf * 512:(hf + 1) * 512],
                                                             start=(c == 0), stop=(c == NCH - 1))
                                    return r
                                return f
                            s.op("pe", mm(0, bg), [slb] + hTb, [self.psb[bg]])
                            s.op("pe", mm(1, bu), [slb] + hTb, [self.psb[bu]])
                            s.op("act", lambda: nc.scalar.activation(out=sg[si][:], in_=self.ps[bg][:, :],
                                                                     func=AF.Silu),
                                 [self.psb[bg]], [sgb[si]])
                            s.op("dve", lambda: nc.vector.tensor_tensor(
                                out=act[:, fc, hf * 512:(hf + 1) * 512], in0=sg[si][:], in1=self.ps[bu][:, :],
                                op=ALU.mult), [sgb[si], self.psb[bu]], [actb[fc]])
                k = 0
                for oc in range(NCH):
                    slot, slb = self.acquire("wd")
                    wd = slot[:, 0:NFC * 128].rearrange("p (c o) -> p c o", c=NFC)
                    xi = self.xin_load(oc, t0, T)
                    for hf in range(2):
                        by = 6 + (k % 2)
                        k += 1

                        def mmd():
                            r = None
                            for fc in range(NFC):
                                r = nc.tensor.matmul(self.ps[by][:, :], lhsT=wd[:, fc, :],
                                                     rhs=act[:, fc, hf * 512:(hf + 1) * 512],
                                                     start=(fc == 0), stop=(fc == NFC - 1))
                            return r
                        s.op("pe", mmd, [slb] + actb, [self.psb[by]])
                        xs = self.xin[xi][:, hf * 512:(hf + 1) * 512]
                        s.op("dve", lambda: nc.vector.scalar_tensor_tensor(
                            out=xs, in0=self.ps[by][:, :], scalar=0.5, in1=xs, op0=ALU.mult, op1=ALU.add),
                            [self.psb[by], self.xinb[xi]], [self.xinb[xi]])
                        self.next_stats(xi, hf * 512, hf, t0 + hf * 512, oc == 0, oc == NCH - 1)
                    self.xin_store(xi, oc, t0, T)
                    if tt == 0:
                        self.emit_norm_chunk(T, T, gi, oc, hT[:, oc, :], hTb[oc])
                self.flush_stats()
            self.have_stats = True
            s.barrier(hTb + actb + sgb)

    def phase_attn(self, l):
        s, nc = self.s, self.nc
        mixer = l % 3
        gi = l * 3 + 1
        SCALE = DH ** -0.5
        ps, psb = self.ps, self.psb
        with contextlib.ExitStack() as ph:
            def sb(name, shape, dt):
                return ph.enter_context(nc.sbuf_tensor(self.nm() + name, shape, dt))
            allb = []

            def mkb(name):
                b = Buf(name)
                allb.append(b)
                return b
            hT = sb("hT_a", [128, 16, S], BF16)
            hTb2 = [[mkb(f"hTa{c}_{t}") for c in range(NCH)] for t in range(2)]
            hTb = hTb2[0] + hTb2[1]
            qT = sb("qT", [128, S], BF16)
            kT = sb("kT", [128, S], BF16)
            vv = sb("vv", [128, 16, 128], BF16)
            qb, kb_, vb = mkb("qT"), mkb("kT"), mkb("vv")
            tmp = [sb(f"tmp{i}", [128, 512], F32) for i in range(6)]
            tmpb = [mkb(f"tmp{i}") for i in range(6)]
            et = [sb(f"et{i}", [128, 512], BF16) for i in range(4)]
            etb = [mkb(f"et{i}") for i in range(4)]
            rden = sb("rden", [128, 512], F32)
            rdb = mkb("rden")
            osb = sb("osb", [128, 512], F32)
            osbb = mkb("osb")
            mst = [sb(f"mst{i}", [128, 512], BF16) for i in range(2)]
            mstb = [mkb("mst0"), mkb("mst1")]
            mst_sem = [s.dma_sem(f"mst_sem{l}_{i}") for i in range(2)]
            aux_sem = s.dma_sem(f"aux_sem{l}")
            mload_sem = s.dma_sem(f"mload_sem{l}")
            mTsb = [mkb(f"mTs{q}") for q in range(4)]
            auxb = mkb("aux")

            for tt in range(2):
                self.emit_norm(tt * 1024, 1024, gi, lambda c: hT[:, c, tt * 1024:(tt + 1) * 1024], hTb2[tt])

            if mixer == 0:
                wf = sb("wf", [128, 16, 16], BF16)
                bfb = sb("bfb", [128, 16, 16], F32)
                nlf = sb("nlf", [128, 256], F32)
                carry = sb("carry", [128, 16, 16], F32)
                P_all = sb("P_all", [128, 16, 16], F32)
                PT = sb("PT", [16, S], F32)
                sel = sb("sel_sb", [16, 2048], F32)
                CB = sb("CB", [128, S], F32)
                cbb = mkb("CB")
                win = self.W_in(l).rearrange("(c p) n -> p c n", p=128)
                s.dma("pool", [(wf[:], win[:, :, 3 * D:3 * D + H])], [], [auxb], aux_sem)
                bf = self.W_small(f"b_f_{l}")
                bsrc = bass.AP(tensor=bf.tensor, offset=0, ap=[[0, 128], [0, 16], [1, 16]])
                s.dma("sp", [(bfb[:], bsrc), (sel[:], self.sel_d)], [], [auxb], aux_sem)

                def fmm():
                    r = None
                    for kb in range(16):
                        for c in range(NCH):
                            r = nc.tensor.matmul(ps[2][:, kb * 16:(kb + 1) * 16],
                                                 lhsT=hT[:, c, kb * 128:(kb + 1) * 128], rhs=wf[:, c, :],
                                                 start=(c == 0), stop=(c == NCH - 1))
                    return r
                s.op("pe", fmm, hTb + [auxb], [psb[2]])
                nb = mkb("nlf")
                s.op("dve", lambda: nc.vector.tensor_tensor(out=nlf[:], in0=ps[2][:, 0:256],
                                                            in1=bfb[:].rearrange("p a b -> p (a b)"),
                                                            op=ALU.add), [psb[2], auxb], [nb])
                s.op("act", lambda: nc.scalar.activation(out=nlf[:], in_=nlf[:], func=AF.Exp, scale=-1.0),
                     [nb], [nb])
                s.op("act", lambda: nc.scalar.activation(out=nlf[:], in_=nlf[:], func=AF.Ln, bias=1.0),
                     [nb], [nb])
                s.op("pe", lambda: nc.tensor.matmul(ps[2][:, 0:256], lhsT=self.cmat(1), rhs=nlf[:],
                                                    start=True, stop=True), [nb, self.cb], [psb[2]])
                s.op("pe", lambda: nc.tensor.matmul(ps[3][:, 0:256], lhsT=self.ones_f[:], rhs=nlf[:],
                                                    start=True, stop=True), [nb, self.cb], [psb[3]])
                cab = mkb("carry")
                s.op("dve", lambda: nc.vector.memset(carry[:, 0, :], 0.0), [], [cab])
                for kb in range(1, 16):
                    s.op("dve", lambda: nc.vector.tensor_tensor(
                        out=carry[:, kb, :], in0=carry[:, kb - 1, :], in1=ps[3][:, (kb - 1) * 16:kb * 16],
                        op=ALU.add), [cab, psb[3]], [cab])
                pab = mkb("P_all")
                s.op("dve", lambda: nc.vector.tensor_tensor(
                    out=P_all[:].rearrange("p a b -> p (a b)"), in0=carry[:].rearrange("p a b -> p (a b)"),
                    in1=ps[2][:, 0:256], op=ALU.add), [cab, psb[2]], [pab])
                ptb = mkb("PT")
                for k4 in range(4):
                    bank = 2 + (k4 % 2)

                    def ptr():
                        r = None
                        for u in range(4):
                            kb = k4 * 4 + u
                            r = nc.tensor.transpose(out=ps[bank][0:16, u * 128:(u + 1) * 128],
                                                    in_=P_all[:, kb, :], identity=self.ident())
                        return r
                    s.op("pe", ptr, [pab, self.cb], [psb[bank]])
                    s.op("act", lambda: nc.scalar.copy(out=PT[0:16, k4 * 512:(k4 + 1) * 512],
                                                       in_=ps[bank][0:16, :]), [psb[bank]], [ptb])
            elif mixer == 1:
                Hh = sb("Hh", [128, S], F32)
                MT = sb("MT", [128, S], F32)
                hhb, mtb = mkb("Hh"), mkb("MT")
                hh_sem = s.dma_sem(f"hh_sem{l}")
                with contextlib.ExitStack() as ph2:
                    ohl = ph2.enter_context(nc.sbuf_tensor(self.nm() + "ohl", [33, NF], F32))
                    fps = ph2.enter_context(nc.sbuf_tensor(self.nm() + "fps", [16, NF], F32))
                    rbT = ph2.enter_context(nc.sbuf_tensor(self.nm() + "rbT", [33, 16], F32))
                    fpb = Buf("fps")
                    s.op("dve", lambda: nc.vector.memset(rbT[:], 1.0), [], [auxb])
                    s.dma("sp", [(ohl[:], self.ohlm_d),
                                 (rbT[0:32, :], self.W_small("rel_bias").rearrange("h b -> b h"))],
                          [], [auxb], aux_sem, allow_slow_non_contiguous=True)
                    for i5 in range(5):
                        c0 = i5 * 512
                        n = min(512, NF - c0)
                        bank = 2 + (i5 % 2)
                        s.op("pe", lambda: nc.tensor.matmul(ps[bank][0:16, 0:n], lhsT=rbT[0:33, :],
                                                            rhs=ohl[0:33, c0:c0 + n], start=True, stop=True),
                             [auxb], [psb[bank]])
                        s.op("act", lambda: nc.scalar.copy(out=fps[0:16, c0:c0 + n], in_=ps[bank][0:16, 0:n]),
                             [psb[bank]], [fpb])
                    s.dma("sp", [(self.fpd, fps[:])], [fpb], [self.fpdb], aux_sem)
                    s.barrier()
            else:
                spf = [sb(f"spf{i}", [128, 512], F32) for i in range(3)]
                spb = [mkb(f"spf{i}") for i in range(3)]
                spe = [sb(f"spe{i}", [128, 512], F32) for i in range(2)]
                speb = [mkb(f"spe{i}") for i in range(2)]
                sph = [sb(f"sph{i}", [128, 512], BF16) for i in range(3)]
                sphb = [mkb(f"sph{i}") for i in range(3)]
                ltb = sb("lt_bf", [128, 128], BF16)
                s.op("dve", lambda: nc.vector.tensor_copy(out=ltb[:], in_=self.cmat(2)), [self.cb], [auxb])
                R = [sb(f"R_sb{i}", [128, 512], F32) for i in range(4)]
                rb = [mkb(f"R{i}") for i in range(4)]

            kproj = 0
            mi = 0
            for h in range(H):
                slot, slb = self.acquire("wqkv")
                wq = slot[:, 0:3 * 16 * 128].rearrange("p (m c n) -> p m c n", m=3, c=16)
                for (m, dst, dbuf, scl) in ((0, qT, qb, SCALE), (1, kT, kb_, 1.0)):
                    for tq in range(4):
                        bank = 2 + (kproj % 2)
                        kproj += 1

                        def pm():
                            r = None
                            for c in range(NCH):
                                r = nc.tensor.matmul(ps[bank][:, :], lhsT=wq[:, m, c, :],
                                                     rhs=hT[:, c, tq * 512:(tq + 1) * 512],
                                                     start=(c == 0), stop=(c == NCH - 1))
                            return r
                        s.op("pe", pm, [slb] + hTb2[tq // 2], [psb[bank]])
                        s.op("act", lambda: nc.scalar.activation(out=dst[:, tq * 512:(tq + 1) * 512],
                                                                 in_=ps[bank][:, :], func=AF.Copy, scale=scl),
                             [psb[bank]], [dbuf])
                for k4 in range(4):
                    bank = 2 + (kproj % 2)
                    kproj += 1

                    def vm():
                        r = None
                        for u in range(4):
                            kb = k4 * 4 + u
                            for c in range(NCH):
                                r = nc.tensor.matmul(ps[bank][:, u * 128:(u + 1) * 128],
                                                     lhsT=hT[:, c, kb * 128:(kb + 1) * 128], rhs=wq[:, 2, c, :],
                                                     start=(c == 0), stop=(c == NCH - 1))
                        return r
                    s.op("pe", vm, [slb] + hTb2[k4 // 2], [psb[bank]])
                    s.op("dve", lambda: nc.vector.tensor_copy(
                        out=vv[:, k4 * 4:(k4 + 1) * 4, :],
                        in_=ps[bank][:, :].rearrange("p (u t) -> p u t", u=4)), [psb[bank]], [vb])
                self.prefetch()
                if mixer == 0:
                    for tq in range(4):
                        bank = 2 + (kproj % 2)
                        kproj += 1
                        s.op("pe", lambda: nc.tensor.matmul(ps[bank][:, :], lhsT=sel[0:16, h * 128:(h + 1) * 128],
                                                            rhs=PT[0:16, tq * 512:(tq + 1) * 512],
                                                            start=True, stop=True), [ptb, auxb], [psb[bank]])
                        s.op("act", lambda: nc.scalar.activation(out=CB[:, tq * 512:(tq + 1) * 512],
                                                                 in_=ps[bank][:, :], func=AF.Copy, scale=-1.0),
                             [psb[bank]], [cbb])
                elif mixer == 1:
                    hsrc = bass.AP(tensor=self.fpd.tensor, offset=h * NF, ap=[[1, 128], [1, S]])
                    s.dma("sp", [(Hh[:], hsrc)], [self.fpdb], [hhb], hh_sem)
                    for tq in range(4):
                        bank = 2 + (kproj % 2)
                        kproj += 1
                        s.op("pe", lambda: nc.tensor.matmul(ps[bank][:, :], lhsT=self.cmat(3),
                                                            rhs=Hh[:, tq * 512:(tq + 1) * 512],
                                                            start=True, stop=True), [hhb, self.cb], [psb[bank]])
                        s.op("act", lambda: nc.scalar.copy(out=MT[:, tq * 512:(tq + 1) * 512], in_=ps[bank][:, :]),
                             [psb[bank]], [mtb])

                steps = []
                for G in range(4):
                    nJ = 4 * G + 4
                    Js = list(range(nJ)) if mixer != 2 else list(range(nJ - 1, -1, -1))
                    for n, J in enumerate(Js):
                        r0 = max(0, J - 4 * G)
                        steps.append((G, n, J, r0 * 128, 512 - r0 * 128, J >= 4 * G, n == 0, n == nJ - 1))
                NS = len(steps)
                stbanks = (0, 1, 4, 5) if mixer != 2 else (2, 4, 5)

                def pe_combo(parts):
                    parts = [p for p in parts if p is not None]
                    if not parts:
                        return

                    def f():
                        r = None
                        for p in parts:
                            r = p[0]()
                        return r
                    rd, wr = [], []
                    for p in parts:
                        rd += p[1]
                        wr += p[2]
                    s.op("pe", f, rd, wr)

                def part_ST(g):
                    if not (0 <= g < NS):
                        return None
                    G, n, J, c0, ncol, diag, first, last = steps[g]
                    bank = stbanks[g % len(stbanks)]
                    return (lambda: nc.tensor.matmul(ps[bank][:, 0:ncol], lhsT=kT[:, J * 128:(J + 1) * 128],
                                                     rhs=qT[:, G * 512 + c0:(G + 1) * 512],
                                                     start=True, stop=True), [kb_, qb], [psb[bank]])

                def part_PV(g, with_den):
                    if not (0 <= g < NS):
                        return None
                    G, n, J, c0, ncol, diag, first, last = steps[g]
                    ei = g % 4

                    def f():
                        r = nc.tensor.matmul(ps[6][:, c0:512], lhsT=vv[:, J, :], rhs=et[ei][:, 0:ncol],
                                             start=first, stop=last)
                        if with_den:
                            r = nc.tensor.matmul(ps[7][:, c0:512], lhsT=self.ones_bf[:], rhs=et[ei][:, 0:ncol],
                                                 start=first, stop=last)
                        return r
                    return (f, [vb, etb[ei], self.cb], [psb[6], psb[7]] if with_den else [psb[6]])

                def store_m(G):
                    nonlocal mi
                    s.dma("sp", [(self.mTd[h * 128:(h + 1) * 128, G * 512:(G + 1) * 512], mst[mi][:])],
                          [mstb[mi]], [self.mTdb[h][G]], mst_sem[mi])
                    mi = 1 - mi

                if mixer != 2:
                    def emit_EW(g):
                        G, n, J, c0, ncol, diag, first, last = steps[g]
                        bank = stbanks[g % 4]
                        ti = g % 6
                        ei = g % 4
                        if mixer == 0:
                            s.op("dve", lambda: nc.vector.scalar_tensor_tensor(
                                out=tmp[ti][:, 0:ncol], in0=ps[bank][:, 0:ncol], scalar=P_all[:, J, h:h + 1],
                                in1=CB[:, G * 512 + c0:(G + 1) * 512], op0=ALU.add, op1=ALU.add),
                                [psb[bank], cbb, pab], [tmpb[ti]])
                            if diag:
                                s.op("dve", lambda: nc.vector.tensor_tensor(
                                    out=tmp[ti][:, 0:128], in0=tmp[ti][:, 0:128], in1=self.cmat(4),
                                    op=ALU.add), [tmpb[ti], self.cb], [tmpb[ti]])
                            s.op("act", lambda: nc.scalar.activation(
                                out=et[ei][:, 0:ncol], in_=tmp[ti][:, 0:ncol], func=AF.Exp),
                                [tmpb[ti]], [etb[ei]])
                        else:
                            mc0 = 128 * (4 * G - J) + c0
                            s.op("dve", lambda: nc.vector.tensor_tensor(
                                out=tmp[ti][:, 0:ncol], in0=ps[bank][:, 0:ncol],
                                in1=MT[:, mc0:mc0 + ncol], op=ALU.add), [psb[bank], mtb], [tmpb[ti]])
                            s.op("act", lambda: nc.scalar.activation(
                                out=et[ei][:, 0:ncol], in_=tmp[ti][:, 0:ncol], func=AF.Exp),
                                [tmpb[ti]], [etb[ei]])
                    SK = 3
                    pe_combo([part_ST(g) for g in range(min(SK, NS))])
                    for g in range(NS):
                        emit_EW(g)
                        pe_combo([part_PV(g, True), part_ST(g + SK)])
                        if steps[g][7]:
                            G = steps[g][0]
                            s.op("act", lambda: nc.scalar.activation(out=rden[:], in_=ps[7][:, :], func=AF.Ln),
                                 [psb[7]], [rdb])
                            s.op("act", lambda: nc.scalar.copy(out=osb[:], in_=ps[6][:, :]), [psb[6]], [osbb])
                            s.op("act", lambda: nc.scalar.activation(out=rden[:], in_=rden[:], func=AF.Exp,
                                                                     scale=-1.0), [rdb], [rdb])
                            s.op("pool", lambda: nc.gpsimd.tensor_tensor(out=mst[mi][:], in0=osb[:], in1=rden[:],
                                                                         op=ALU.mult), [osbb, rdb], [mstb[mi]])
                            store_m(G)
                else:
                    def Rof(G, n):
                        return 2 * (G % 2) + (n % 2)

                    def S1(g):
                        G, n, J, c0, ncol, diag, first, last = steps[g]
                        bank = stbanks[g % 3]
                        si = g % 2
                        s.op("act", lambda: nc.scalar.activation(out=spe[si][:, 0:ncol], in_=ps[bank][:, 0:ncol],
                                                                 func=AF.Exp), [psb[bank]], [speb[si]])
                        s.op("act", lambda: nc.scalar.activation(out=spf[si][:, 0:ncol], in_=spe[si][:, 0:ncol],
                                                                 func=AF.Ln, bias=1.0), [speb[si]], [spb[si]])
                        hi = g % 3
                        s.op("act", lambda: nc.scalar.activation(out=sph[hi][:, 0:ncol], in_=spe[si][:, 0:ncol],
                                                                 func=AF.Ln, bias=1.0), [speb[si]], [sphb[hi]])

                    def S2(g):
                        G, n, J, c0, ncol, diag, first, last = steps[g]
                        bank = stbanks[g % 3]
                        si = g % 2
                        ti = g % 6
                        if first:
                            for rr in range(2):
                                ri = 2 * (G % 2) + rr
                                s.op("dve", lambda: nc.vector.memset(R[ri][:], 0.0), [], [rb[ri]])
                        if diag:
                            hi = g % 3
                            s.op("pool", lambda: nc.gpsimd.tensor_tensor(
                                out=sph[hi][:, 0:128], in0=spf[si][:, 0:128], in1=self.cmat(5),
                                op=ALU.mult), [spb[si], self.cb], [sphb[hi]])
                        s.op("dve", lambda: nc.vector.tensor_tensor(
                            out=tmp[ti][:, 0:ncol], in0=ps[bank][:, 0:ncol], in1=spf[si][:, 0:ncol],
                            op=ALU.subtract), [psb[bank], spb[si]], [tmpb[ti]])

                    def part_WC(g):
                        if not (0 <= g < NS):
                            return None
                        G, n, J, c0, ncol, diag, first, last = steps[g]
                        si = g % 3
                        wb = g % 2
                        cbk = (3, 7)[g % 2]

                        def f():
                            nc.tensor.matmul(ps[wb][:, 0:ncol], lhsT=ltb[:], rhs=sph[si][:, 0:ncol],
                                             start=True, stop=True)
                            return nc.tensor.matmul(ps[cbk][:, 0:ncol], lhsT=self.ones_bf[:],
                                                    rhs=sph[si][:, 0:ncol], start=True, stop=True)
                        return (f, [sphb[si], self.cb, auxb], [psb[wb], psb[cbk]])

                    def S4(g):
                        G, n, J, c0, ncol, diag, first, last = steps[g]
                        wb = g % 2
                        cbk = (3, 7)[g % 2]
                        ti = g % 6
                        rc, rn = Rof(G, n), Rof(G, n + 1)
                        s.op("dve", lambda: nc.vector.tensor_tensor(
                            out=tmp[ti][:, 0:ncol], in0=tmp[ti][:, 0:ncol], in1=ps[wb][:, 0:ncol],
                            op=ALU.subtract), [tmpb[ti], psb[wb]], [tmpb[ti]])
                        if not last:
                            s.op("dve", lambda: nc.vector.tensor_tensor(
                                out=R[rn][:, c0:512], in0=R[rc][:, c0:512], in1=ps[cbk][:, 0:ncol],
                                op=ALU.add), [rb[rc], psb[cbk]], [rb[rn]])

                    def S5(g):
                        G, n, J, c0, ncol, diag, first, last = steps[g]
                        ti = g % 6
                        rc = Rof(G, n)
                        s.op("pool", lambda: nc.gpsimd.tensor_tensor(
                            out=tmp[ti][:, 0:ncol], in0=tmp[ti][:, 0:ncol], in1=R[rc][:, c0:512],
                            op=ALU.subtract), [tmpb[ti], rb[rc]], [tmpb[ti]])
                        if diag:
                            s.op("pool", lambda: nc.gpsimd.tensor_tensor(
                                out=tmp[ti][:, 0:128], in0=tmp[ti][:, 0:128], in1=self.cmat(6),
                                op=ALU.add), [tmpb[ti], self.cb], [tmpb[ti]])

                    def S6(g):
                        G, n, J, c0, ncol, diag, first, last = steps[g]
                        ti = g % 6
                        ei = g % 4
                        s.op("act", lambda: nc.scalar.activation(
                            out=et[ei][:, 0:ncol], in_=tmp[ti][:, 0:ncol], func=AF.Exp),
                            [tmpb[ti]], [etb[ei]])
                    nonpe = {1: S1, 2: S2, 4: S4, 5: S5, 6: S6}
                    for it in range(NS + 7):
                        for kk in (6, 5, 4, 2, 1):
                            g = it - kk
                            if 0 <= g < NS:
                                nonpe[kk](g)
                        pe_combo([part_PV(it - 7, False), part_WC(it - 3), part_ST(it)])
                        g7 = it - 7
                        if 0 <= g7 < NS and steps[g7][7]:
                            s.op("act", lambda: nc.scalar.copy(out=mst[mi][:], in_=ps[6][:, :]), [psb[6]], [mstb[mi]])
                            store_m(steps[g7][0])

            mTv = self.mTd.rearrange("(c p) t -> p c t", p=128)
            for tq in range(4):
                s.dma("sp", [(hT[:, :, tq * 512:(tq + 1) * 512], mTv[:, :, tq * 512:(tq + 1) * 512])],
                      [self.mTdb[c][tq] for c in range(NCH)], [mTsb[tq]] + hTb, mload_sem)
            k = 0
            for og in range(4):
                slot, slb = self.acquire("wo")
                wo = slot[:, 0:8192].rearrange("p (c o) -> p c o", c=16)
                for tq in range(4):
                    for ocl in range(4):
                        oc = og * 4 + ocl
                        bank = 2 + (k % 2)
                        k += 1
                        xi = self.xin_load(oc, tq * 512, 512)

                        def om():
                            r = None
                            for c in range(NCH):
                                r = nc.tensor.matmul(ps[bank][:, :], lhsT=wo[:, c, ocl * 128:(ocl + 1) * 128],
                                                     rhs=hT[:, c, tq * 512:(tq + 1) * 512],
                                                     start=(c == 0), stop=(c == NCH - 1))
                            return r
                        s.op("pe", om, [slb, mTsb[tq]], [psb[bank]])
                        s.op("dve", lambda: nc.vector.tensor_tensor(
                            out=self.xin[xi][:, 0:512], in0=ps[bank][:, :], in1=self.xin[xi][:, 0:512],
                            op=ALU.add), [psb[bank], self.xinb[xi]], [self.xinb[xi]])
                        self.next_stats(xi, 0, 4 + tq, tq * 512, oc == 0, oc == NCH - 1)
                        self.xin_store(xi, oc, tq * 512, 512)
            self.flush_stats()
            self.have_stats = True
            s.barrier(allb)


def gather_inputs(prog, x_c, ws, consts):
    m = {"x": x_c}
    for name in prog.dram_in:
        if "@" in name:
            parts = name.split("@")
            a = ws[parts[0]]
            for p_ in parts[1:]:
                a = a[int(p_)]
            m[name] = np.ascontiguousarray(a)
        else:
            m[name] = ws[name]
    m.update(consts)
    return m


_CACHE = {}


def kernel(**inputs):
    consts = make_consts()
    x = np.ascontiguousarray(np.asarray(inputs["x"], dtype=np.float32))
    ws = {n: np.ascontiguousarray(np.asarray(inputs[n], dtype=np.float32)) for n in W_NAMES}
    if "p" not in _CACHE:
        _CACHE["p"] = Prog()
    prog = _CACHE["p"]
    in_maps = [gather_inputs(prog, x[c], ws, consts) for c in range(NCORES)]
    res = run_bass_kernel_spmd(prog.nc, in_maps, core_ids=list(range(NCORES)))
    return np.stack([np.asarray(r["y"]) for r in res.results], axis=0).astype(np.float32)
```

```python
import contextlib
import math
import numpy as np
import concourse.bass as bass
import concourse.mybir as mybir
from concourse.bass_utils import run_bass_kernel_spmd

F32 = mybir.dt.float32
BF16 = mybir.dt.bfloat16
AF = mybir.ActivationFunctionType
ALU = mybir.AluOpType

S = 2048
D = 2048
FF = 5632
H = 16
DH = 128
DEPTH = 4
NCH = 16
NFC = 44
EPS = 1e-6
NEG = -30000.0
NCORES = 8
NF = 2176

W_NAMES = ["norm_g", "ffn_w_gate", "ffn_w_up", "ffn_w_down", "w_in_0", "b_f_0", "w_in_1",
           "w_in_2", "w_in_3", "b_f_3", "w_out", "rel_bias", "final_g"]
W_SHAPES = {
    "norm_g": [DEPTH, 3, D], "ffn_w_gate": [DEPTH, 2, D, FF], "ffn_w_up": [DEPTH, 2, D, FF],
    "ffn_w_down": [DEPTH, 2, FF, D], "w_in_0": [D, 3 * D + H], "b_f_0": [H],
    "w_in_1": [D, 3 * D], "w_in_2": [D, 3 * D], "w_in_3": [D, 3 * D + H], "b_f_3": [H],
    "w_out": [DEPTH, D, D], "rel_bias": [H, 32], "final_g": [D],
}


def _t5_bucket_np(dist):
    dist = np.asarray(dist, dtype=np.int64)
    d = np.maximum(dist, 1).astype(np.float32)
    large = 16 + (np.log(d / np.float32(16.0)) / np.float32(math.log(2048 / 16)) * np.float32(16.0)).astype(np.int32)
    large = np.minimum(large, 31)
    return np.where(dist < 16, dist, large).astype(np.int64)


def make_consts():
    i = np.arange(128)
    c128 = np.zeros((7, 128, 128), np.float32)
    c128[0] = np.eye(128)
    c128[1] = (i[:, None] <= i[None, :])
    c128[2] = (i[:, None] > i[None, :])
    c128[3] = (i[:, None] + i[None, :] == 127)
    c128[4] = np.where(i[:, None] <= i[None, :], 0.0, NEG)
    c128[5] = (i[:, None] < i[None, :])
    c128[6] = np.where(i[:, None] < i[None, :], 0.0, NEG)
    c128 = np.ascontiguousarray(c128.transpose(1, 0, 2).reshape(128, 7 * 128))
    sel = np.zeros((16, 16 * 128), np.float32)
    for h in range(16):
        sel[h, h * 128:(h + 1) * 128] = 1.0
    ohlm = np.zeros((33, NF), np.float32)
    for idx in range(NF):
        dl = idx - 127
        if dl < 0 or dl > 2048:
            ohlm[32, idx] = NEG
            continue
        mult = (1 if dl <= 128 else 0) + (1 if (dl % 4 == 0 and dl <= 512) else 0) + \
               (1 if (dl % 16 == 0 and dl <= 2048) else 0)
        if mult == 0:
            ohlm[32, idx] = NEG
        else:
            ohlm[32, idx] = math.log(mult)
            ohlm[int(_t5_bucket_np(dl)), idx] = 1.0
    return {"c128": c128, "sel": sel, "ohlm": ohlm}


class Buf:
    __slots__ = ("name", "w", "r")

    def __init__(self, name):
        self.name = name
        self.w = None
        self.r = {}


class Sched:
    ENG = ("pe", "act", "dve", "pool", "sp")

    def __init__(self, nc, es):
        self.nc = nc
        self.es = es
        self.e = {"pe": nc.tensor, "act": nc.scalar, "dve": nc.vector, "pool": nc.gpsimd, "sp": nc.sync}
        self.sem = {k: es.enter_context(nc.semaphore("sem_" + k)) for k in self.ENG}
        self.cnt = {k: 0 for k in self.ENG}
        self.seen = {k: {} for k in self.ENG}
        self.dsems = []
        self.same_sync = True
        self.nwait = 0
        self.nops = 0

    def dma_sem(self, name):
        s = self.es.enter_context(self.nc.semaphore(name))
        box = [s, 0]
        self.dsems.append(box)
        return box

    def _wait(self, eng, sem, val):
        k = id(sem)
        if self.seen[eng].get(k, 0) < val:
            self.e[eng].wait_ge(sem, val)
            self.seen[eng][k] = val
            self.nwait += 1

    def _need(self, eng, reads, writes):
        need = {}

        def add(tok):
            s, v, owner = tok
            if owner == eng and (eng in ("pe", "sp") or not self.same_sync):
                return
            k = id(s)
            if k not in need or need[k][1] < v:
                need[k] = (s, v)

        for b in reads:
            if b.w is not None:
                add(b.w)
        for b in writes:
            if b.w is not None:
                add(b.w)
            for t in b.r.values():
                add(t)
        for s, v in need.values():
            self._wait(eng, s, v)

    def _commit(self, tok, reads, writes):
        for b in writes:
            b.w = tok
            b.r = {}
        k = id(tok[0])
        for b in reads:
            if b in writes:
                continue
            o = b.r.get(k)
            if o is None or o[1] < tok[1]:
                b.r[k] = tok

    def op(self, eng, fn, reads=(), writes=()):
        self._need(eng, reads, writes)
        inst = fn()
        self.cnt[eng] += 1
        inst.then_inc(self.sem[eng], 1)
        self.nops += 1
        self._commit((self.sem[eng], self.cnt[eng], eng), reads, writes)

    def dma(self, q, pairs, reads, writes, box, **kw):
        self._need(q, reads, writes)
        for (o, i) in pairs:
            inst = self.e[q].dma_start(out=o, in_=i, **kw)
            box[1] += 16
            inst.then_inc(box[0], 16)
        self._commit((box[0], box[1], None), reads, writes)

    def barrier(self, bufs_to_reset=()):
        for eng in self.ENG:
            for k in self.ENG:
                if self.cnt[k] > 0 and not (k == eng and eng in ("pe", "sp")):
                    self._wait(eng, self.sem[k], self.cnt[k])
            for box in self.dsems:
                if box[1] > 0:
                    self._wait(eng, box[0], box[1])
        for b in bufs_to_reset:
            b.w = None
            b.r = {}


class Prog:
    def __init__(self, layers=(0, 1, 2, 3), subs=("ffn1", "attn", "ffn2"), do_in=True, do_out=True,
                 compact=False):
        self.compact = compact
        self.layers = list(layers)
        self.subs = tuple(subs)
        self.do_in = do_in
        self.do_out = do_out
        self.nc = bass.Bass("TRN2", target_bir_lowering=False)
        self.build()

    def build(self):
        nc = self.nc
        with contextlib.ExitStack() as es:
            self.es = es
            s = self.s = Sched(nc, es)
            self.x = nc.dram_tensor("x", [S, D], F32, kind="ExternalInput").ap()
            self.dram_in = {}
            self.c128_d = nc.dram_tensor("c128", [128, 7 * 128], F32, kind="ExternalInput").ap()
            self.sel_d = nc.dram_tensor("sel", [16, 2048], F32, kind="ExternalInput").ap()
            self.ohlm_d = nc.dram_tensor("ohlm", [33, NF], F32, kind="ExternalInput").ap()
            self.y = nc.dram_tensor("y", [S, D], F32, kind="ExternalOutput").ap()
            self.xT = nc.dram_tensor("xT_scr", [D, S], F32, kind="Internal").ap()
            self.mTd = nc.dram_tensor("mT_scr", [D, S], BF16, kind="Internal").ap()
            self.fpd = nc.dram_tensor("fp_scr", [16, NF], F32, kind="Internal").ap()
            self.xTb = [[Buf(f"xT{c}_{q}") for q in range(4)] for c in range(NCH)]
            self.mTdb = [[Buf(f"mTd{c}_{q}") for q in range(4)] for c in range(NCH)]
            self.fpdb = Buf("fpd")

            def sb(name, shape, dt):
                return es.enter_context(nc.sbuf_tensor(self.nm() + name, shape, dt))

            self.ps = [es.enter_context(nc.psum_tensor(f"ps{i}", [128, 512], F32)) for i in range(8)]
            self.psb = [Buf(f"ps{i}") for i in range(8)]
            self.c128 = sb("c128_sb", [128, 7 * 128], F32)
            self.ones_bf = sb("ones_bf", [128, 128], BF16)
            self.ones_f = sb("ones_f", [128, 128], F32)
            self.gcol = sb("gcol", [128, 208], F32)
            self.eps_t = sb("eps_t", [128, 1], F32)
            self.rstd = sb("rstd", [128, S], F32)
            self.rstdq = [Buf(f"rstd{q}") for q in range(4)]
            self.have_stats = False
            self.pend_stats = None
            self.sq = [sb(f"sq{i}", [128, 1024], BF16) for i in range(2)]
            self.sqb = [Buf(f"sq{i}") for i in range(2)]
            self.sqh = [Buf(f"sqh{i}") for i in range(4)]
            self.xin = [sb(f"xin{i}", [128, 1024], F32) for i in range(3)]
            self.xinb = [Buf(f"xin{i}") for i in range(3)]
            self.xin_sem = [s.dma_sem(f"xin_sem{i}") for i in range(3)]
            self.xin_i = 0
            self.NSLOT = 2
            self.wsl = [sb(f"wsl{i}", [128, 8192], BF16) for i in range(self.NSLOT)]
            self.wslb = [Buf(f"wsl{i}") for i in range(self.NSLOT)]
            self.wsl_sem = [s.dma_sem(f"wsl_sem{i}") for i in range(self.NSLOT)]
            self.misc_sem = s.dma_sem("misc_sem")
            self.cb = Buf("consts")

            self.jobs = list(self.plan_jobs())
            self.job_issued = 0
            self.job_next = 0

            self.setup_consts()
            if self.do_in:
                self.phase_in()
            for l in self.layers:
                if "ffn1" in self.subs:
                    self.phase_ffn(l, 0)
                if "attn" in self.subs:
                    self.phase_attn(l)
                if "ffn2" in self.subs:
                    self.phase_ffn(l, 1)
            if self.do_out:
                self.phase_out()
            assert self.job_next == len(self.jobs), (self.job_next, len(self.jobs))
            s.barrier()

    def nm(self):
        self._uid = getattr(self, "_uid", 0) + 1
        return f"u{self._uid}_"

    def din(self, name, shape):
        if name not in self.dram_in:
            self.dram_in[name] = self.nc.dram_tensor(name, list(shape), F32, kind="ExternalInput").ap()
        return self.dram_in[name]

    def W_ff(self, nm, l, j):
        shp = W_SHAPES[nm]
        if self.compact:
            return self.din(f"{nm}@{l}@{j}", shp[2:])
        return self.din(nm, shp)[l, j]

    def W_out(self, l):
        if self.compact:
            return self.din(f"w_out@{l}", [D, D])
        return self.din("w_out", W_SHAPES["w_out"])[l]

    def W_in(self, l):
        return self.din(f"w_in_{l}", W_SHAPES[f"w_in_{l}"])

    def W_small(self, nm):
        return self.din(nm, W_SHAPES[nm])

    def ident(self):
        return self.c128[:, 0:128]

    def cmat(self, i):
        return self.c128[:, i * 128:(i + 1) * 128]

    def setup_consts(self):
        s, nc = self.s, self.nc
        s.dma("sp", [(self.c128[:], self.c128_d)], [], [self.cb], self.misc_sem)
        s.op("dve", lambda: nc.vector.memset(self.ones_bf[:], 1.0), [], [self.cb])
        s.op("dve", lambda: nc.vector.memset(self.ones_f[:], 1.0), [], [self.cb])
        s.op("dve", lambda: nc.vector.memset(self.eps_t[:], EPS), [], [self.cb])
        with contextlib.ExitStack() as ph:
            ga = ph.enter_context(nc.sbuf_tensor(self.nm() + "ga", [128, 128], F32))
            gb = ph.enter_context(nc.sbuf_tensor(self.nm() + "gb", [80, 128], F32))
            gbuf = Buf("gab")
            ng = self.W_small("norm_g").rearrange("l j (c p) -> (l j c) p", p=128)
            fg = self.W_small("final_g").rearrange("(c p) -> c p", p=128)
            s.dma("sp", [(ga[:], ng[0:128, :]), (gb[0:64, :], ng[128:192, :]), (gb[64:80, :], fg)],
                  [], [gbuf], self.misc_sem)
            p0 = self.ps[0]
            s.op("pe", lambda: nc.tensor.transpose(out=p0[:, 0:128], in_=ga[:], identity=self.ident()),
                 [gbuf, self.cb], [self.psb[0]])
            s.op("pe", lambda: nc.tensor.transpose(out=p0[:, 128:208], in_=gb[0:80, :],
                                                   identity=self.c128[0:80, 0:80]),
                 [gbuf, self.cb], [self.psb[0]])
            s.op("dve", lambda: nc.vector.tensor_copy(out=self.gcol[:], in_=p0[:, 0:208]),
                 [self.psb[0]], [self.cb])
            s.barrier()

    def plan_jobs(self):
        for l in self.layers:
            if "ffn1" in self.subs:
                yield from self.ffn_jobs(l, 0)
            if "attn" in self.subs:
                for h in range(H):
                    yield ("wqkv", l, h)
                for og in range(4):
                    yield ("wo", l, og)
            if "ffn2" in self.subs:
                yield from self.ffn_jobs(l, 1)

    def ffn_jobs(self, l, j):
        for tt in range(2):
            for fg in range(22):
                yield ("wgu", l, j, fg)
            for oc in range(16):
                yield ("wd", l, j, oc)

    def issue_job(self, idx):
        job = self.jobs[idx]
        si = idx % self.NSLOT
        slot = self.wsl[si]
        kind = job[0]
        pairs = []
        if kind == "wgu":
            _, l, j, fg = job
            dst = slot[:, 0:8192].rearrange("p (m c f) -> p m c f", m=2, c=16)
            for m, nm in enumerate(("ffn_w_gate", "ffn_w_up")):
                src = self.W_ff(nm, l, j).rearrange("(c p) f -> p c f", p=128)[:, :, fg * 256:(fg + 1) * 256]
                pairs.append((dst[:, m], src))
        elif kind == "wd":
            _, l, j, oc = job
            dst = slot[:, 0:NFC * 128].rearrange("p (c o) -> p c o", c=NFC)
            src = self.W_ff("ffn_w_down", l, j).rearrange("(c p) o -> p c o", p=128)[:, :, oc * 128:(oc + 1) * 128]
            pairs.append((dst, src))
        elif kind == "wqkv":
            _, l, h = job
            dst = slot[:, 0:3 * 16 * 128].rearrange("p (m c n) -> p m c n", m=3, c=16)
            win = self.W_in(l).rearrange("(c p) n -> p c n", p=128)
            for m in range(3):
                pairs.append((dst[:, m], win[:, :, m * D + h * 128: m * D + (h + 1) * 128]))
        elif kind == "wo":
            _, l, og = job
            dst = slot[:, 0:8192].rearrange("p (c o) -> p c o", c=16)
            src = self.W_out(l).rearrange("(c p) o -> p c o", p=128)[:, :, og * 512:(og + 1) * 512]
            pairs.append((dst, src))
        self.s.dma("pool", pairs, [], [self.wslb[si]], self.wsl_sem[si])

    def acquire(self, kind):
        idx = self.job_next
        assert self.jobs[idx][0] == kind, (self.jobs[idx], kind)
        while self.job_issued < min(len(self.jobs), idx + self.NSLOT):
            self.issue_job(self.job_issued)
            self.job_issued += 1
        self.job_next += 1
        si = idx % self.NSLOT
        return self.wsl[si], self.wslb[si]

    def prefetch(self):
        idx = self.job_next
        while self.job_issued < min(len(self.jobs), idx + self.NSLOT - 1):
            self.issue_job(self.job_issued)
            self.job_issued += 1

    def xT_bufs(self, c, t0, T):
        return [self.xTb[c][q] for q in range(t0 // 512, (t0 + T) // 512)]

    def xin_load(self, c, t0, T):
        i = self.xin_i
        self.xin_i = (i + 1) % 3
        self.s.dma("sp", [(self.xin[i][:, 0:T], self.xT[c * 128:(c + 1) * 128, t0:t0 + T])],
                   self.xT_bufs(c, t0, T), [self.xinb[i]], self.xin_sem[i])
        return i

    def xin_store(self, i, c, t0, T):
        self.s.dma("sp", [(self.xT[c * 128:(c + 1) * 128, t0:t0 + T], self.xin[i][:, 0:T])],
                   [self.xinb[i]], self.xT_bufs(c, t0, T), self.xin_sem[i])

    def rstd_bufs(self, t0, T):
        return [self.rstdq[q] for q in range(t0 // 512, (t0 + T) // 512)]

    def finish_stats(self, bank, t0):
        s, nc = self.s, self.nc
        sl = self.rstd[:, t0:t0 + 512]
        rb_ = self.rstdq[t0 // 512]
        s.op("act", lambda: nc.scalar.activation(out=sl, in_=self.ps[bank][:, :], func=AF.Ln,
                                                 scale=1.0 / D, bias=self.eps_t[:, 0:1]),
             [self.psb[bank], self.cb], [rb_])
        s.op("act", lambda: nc.scalar.activation(out=sl, in_=sl, func=AF.Exp, scale=-0.5), [rb_], [rb_])

    def emit_stats(self, t0, T):
        s, nc = self.s, self.nc
        nh = T // 512
        for c in range(NCH):
            xi = self.xin_load(c, t0, T)
            qi = c % 2
            s.op("act", lambda: nc.scalar.activation(out=self.sq[qi][:, 0:T], in_=self.xin[xi][:, 0:T],
                                                     func=AF.Square),
                 [self.xinb[xi]], [self.sqb[qi]])

            def mm():
                r = None
                for hf in range(nh):
                    r = nc.tensor.matmul(self.ps[hf][:, :], lhsT=self.ones_bf[:],
                                         rhs=self.sq[qi][:, hf * 512:(hf + 1) * 512],
                                         start=(c == 0), stop=(c == NCH - 1))
                return r
            s.op("pe", mm, [self.sqb[qi], self.cb], [self.psb[hf] for hf in range(nh)])
        for hf in range(nh):
            self.finish_stats(hf, t0 + hf * 512)

    def next_stats(self, xi, col0, bank, t0, first, last):
        s, nc = self.s, self.nc
        self.sq_i = (getattr(self, "sq_i", 0) + 1) % 4
        qi, half = self.sq_i // 2, self.sq_i % 2
        sqv = self.sq[qi][:, half * 512:(half + 1) * 512]
        sqbuf = self.sqh[self.sq_i]
        s.op("act", lambda: nc.scalar.activation(out=sqv, in_=self.xin[xi][:, col0:col0 + 512], func=AF.Square),
             [self.xinb[xi]], [sqbuf, self.sqb[qi]])
        self.flush_stats()
        self.pend_stats = (sqv, sqbuf, bank, t0, first, last)

    def flush_stats(self):
        s, nc = self.s, self.nc
        if self.pend_stats is None:
            return
        sqv, sqbuf, bank, t0, first, last = self.pend_stats
        self.pend_stats = None
        s.op("pe", lambda: nc.tensor.matmul(self.ps[bank][:, :], lhsT=self.ones_bf[:], rhs=sqv,
                                            start=first, stop=last), [sqbuf, self.cb], [self.psb[bank]])
        if last:
            self.finish_stats(bank, t0)

    def emit_norm_chunk(self, t0, T, gi, c, dst, dbuf):
        s, nc = self.s, self.nc
        xi = self.xin_load(c, t0, T)
        s.op("dve", lambda: nc.vector.scalar_tensor_tensor(
            out=dst, in0=self.xin[xi][:, 0:T], scalar=self.gcol[:, gi * 16 + c: gi * 16 + c + 1],
            in1=self.rstd[:, t0:t0 + T], op0=ALU.mult, op1=ALU.mult),
            [self.xinb[xi], self.cb] + self.rstd_bufs(t0, T), [dbuf])

    def emit_norm(self, t0, T, gi, dst_fn, dst_bufs):
        s, nc = self.s, self.nc
        if not self.have_stats:
            self.emit_stats(t0, T)
        for c in range(NCH):
            xi = self.xin_load(c, t0, T)
            s.op("dve", lambda: nc.vector.scalar_tensor_tensor(
                out=dst_fn(c), in0=self.xin[xi][:, 0:T], scalar=self.gcol[:, gi * 16 + c: gi * 16 + c + 1],
                in1=self.rstd[:, t0:t0 + T], op0=ALU.mult, op1=ALU.mult),
                [self.xinb[xi], self.cb] + self.rstd_bufs(t0, T), [dst_bufs[c]])

    def phase_in(self):
        s, nc = self.s, self.nc
        with contextlib.ExitStack() as ph:
            xrow = [ph.enter_context(nc.sbuf_tensor(self.nm() + f"xrow{i}", [128, 2048], F32)) for i in range(2)]
            xcol = [ph.enter_context(nc.sbuf_tensor(self.nm() + f"xcol{i}", [128, 16, 128], F32)) for i in range(2)]
            xrb = [Buf("xrow0"), Buf("xrow1")]
            xcb = [Buf("xcol0"), Buf("xcol1")]
            sem_r = [s.dma_sem("xrow_sem0"), s.dma_sem("xrow_sem1")]
            sem_c = [s.dma_sem("xcol_sem0"), s.dma_sem("xcol_sem1")]
            xTv = self.xT.rearrange("(c p) t -> p c t", p=128)
            k = 0
            for tb in range(16):
                i = tb % 2
                s.dma("sp", [(xrow[i][:], self.x[tb * 128:(tb + 1) * 128, :])], [], [xrb[i]], sem_r[i])
                for c4 in range(4):
                    bank = 2 + (k % 2)
                    k += 1

                    def tr():
                        r = None
                        for u in range(4):
                            cc = c4 * 4 + u
                            r = nc.tensor.transpose(out=self.ps[bank][:, u * 128:(u + 1) * 128],
                                                    in_=xrow[i][:, cc * 128:(cc + 1) * 128],
                                                    identity=self.ident())
                        return r
                    s.op("pe", tr, [xrb[i], self.cb], [self.psb[bank]])
                    s.op("act", lambda: nc.scalar.copy(
                        out=xcol[i][:, c4 * 4:(c4 + 1) * 4, :],
                        in_=self.ps[bank][:, :].rearrange("p (u t) -> p u t", u=4)),
                        [self.psb[bank]], [xcb[i]])
                s.dma("sp", [(xTv[:, :, tb * 128:(tb + 1) * 128], xcol[i][:])], [xcb[i]],
                      [self.xTb[c][tb // 4] for c in range(NCH)], sem_c[i])
            s.barrier(xrb + xcb)

    def phase_out(self):
        s, nc = self.s, self.nc
        with contextlib.ExitStack() as ph:
            yn = ph.enter_context(nc.sbuf_tensor(self.nm() + "yn", [128, 16, 512], F32))
            ynb = [Buf(f"yn{c}") for c in range(NCH)]
            yrow = [ph.enter_context(nc.sbuf_tensor(self.nm() + f"yrow{i}", [128, 2048], F32)) for i in range(2)]
            yrb = [Buf("yrow0"), Buf("yrow1")]
            sem_y = [s.dma_sem("yrow_sem0"), s.dma_sem("yrow_sem1")]
            k = 0
            kk = 0
            for tq in range(4):
                self.emit_norm(tq * 512, 512, 12, lambda c: yn[:, c, :], ynb)
                for tb4 in range(4):
                    i = kk % 2
                    kk += 1
                    for c4 in range(4):
                        bank = 2 + (k % 2)
                        k += 1

                        def tr():
                            r = None
                            for u in range(4):
                                cc = c4 * 4 + u
                                r = nc.tensor.transpose(out=self.ps[bank][:, u * 128:(u + 1) * 128],
                                                        in_=yn[:, cc, tb4 * 128:(tb4 + 1) * 128],
                                                        identity=self.ident())
                            return r
                        s.op("pe", tr, [ynb[c4 * 4 + u] for u in range(4)] + [self.cb], [self.psb[bank]])
                        s.op("act", lambda: nc.scalar.copy(out=yrow[i][:, c4 * 512:(c4 + 1) * 512],
                                                           in_=self.ps[bank][:, :]),
                             [self.psb[bank]], [yrb[i]])
                    r0 = tq * 512 + tb4 * 128
                    s.dma("sp", [(self.y[r0:r0 + 128, :], yrow[i][:])], [yrb[i]], [], sem_y[i])
            s.barrier(ynb + yrb)

    def phase_ffn(self, l, j):
        s, nc = self.s, self.nc
        gi = l * 3 + (0 if j == 0 else 2)
        T = 1024
        with contextlib.ExitStack() as ph:
            hT = ph.enter_context(nc.sbuf_tensor(self.nm() + "hT_f", [128, 16, T], BF16))
            hTb = [Buf(f"hTf{c}") for c in range(NCH)]
            act = ph.enter_context(nc.sbuf_tensor(self.nm() + "act_f", [128, NFC, T], BF16))
            actb = [Buf(f"act{c}") for c in range(NFC)]
            sg = [ph.enter_context(nc.sbuf_tensor(self.nm() + f"sg{i}", [128, 512], F32)) for i in range(2)]
            sgb = [Buf("sg0"), Buf("sg1")]
            if not self.have_stats:
                self.emit_stats(0, T)
                self.emit_stats(T, T)
                self.have_stats = True
            self.emit_norm(0, T, gi, lambda c: hT[:, c, :], hTb)
            for tt in range(2):
                t0 = tt * T
                k = 0
                for fg in range(22):
                    slot, slb = self.acquire("wgu")
                    wv = slot[:, 0:8192].rearrange("p (m c f) -> p m c f", m=2, c=16)
                    for fcl in range(2):
                        fc = fg * 2 + fcl
                        for hf in range(2):
                            bg = 2 + (k % 2)
                            bu = 4 + (k % 2)
                            si = k % 2
                            k += 1

                            def mm(m, bank):
                                def f():
                                    r = None
                                    for c in range(NCH):
                                        r = nc.tensor.matmul(self.ps[bank][:, :],
                                                             lhsT=wv[:, m, c, fcl * 128:(fcl + 1) * 128],
                                                             rhs=hT[:, c, hf * 512:(hf + 1) * 512],
                                                             start=(c == 0), stop=(c == NCH - 1))
                                    return r
                                return f
                            s.op("pe", mm(0, bg), [slb] + hTb, [self.psb[bg]])
                            s.op("pe", mm(1, bu), [slb] + hTb, [self.psb[bu]])
                            s.op("act", lambda: nc.scalar.activation(out=sg[si][:], in_=self.ps[bg][:, :],
                                                                     func=AF.Silu),
                                 [self.psb[bg]], [sgb[si]])
                            s.op("dve", lambda: nc.vector.tensor_tensor(
                                out=act[:, fc, hf * 512:(hf + 1) * 512], in0=sg[si][:], in1=self.ps[bu][:, :],
                                op=ALU.mult), [sgb[si], self.psb[bu]], [actb[fc]])
                k = 0
                for oc in range(NCH):
                    slot, slb = self.acquire("wd")
                    wd = slot[:, 0:NFC * 128].rearrange("p (c o) -> p c o", c=NFC)
                    xi = self.xin_load(oc, t0, T)
                    for hf in range(2):
                        by = 6 + (k % 2)
                        k += 1

                        def mmd():
                            r = None
                            for fc in range(NFC):
                                r = nc.tensor.matmul(self.ps[by][:, :], lhsT=wd[:, fc, :],
                                                     rhs=act[:, fc, hf * 512:(hf + 1) * 512],
                                                     start=(fc == 0), stop=(fc == NFC - 1))
                            return r
                        s.op("pe", mmd, [slb] + actb, [self.psb[by]])
                        xs = self.xin[xi][:, hf * 512:(hf + 1) * 512]
                        s.op("dve", lambda: nc.vector.scalar_tensor_tensor(
                            out=xs, in0=self.ps[by][:, :], scalar=0.5, in1=xs, op0=ALU.mult, op1=ALU.add),
                            [self.psb[by], self.xinb[xi]], [self.xinb[xi]])
                        self.next_stats(xi, hf * 512, hf, t0 + hf * 512, oc == 0, oc == NCH - 1)
                    self.xin_store(xi, oc, t0, T)
                    if tt == 0:
                        self.emit_norm_chunk(T, T, gi, oc, hT[:, oc, :], hTb[oc])
                self.flush_stats()
            self.have_stats = True
            s.barrier(hTb + actb + sgb)

    def phase_attn(self, l):
        s, nc = self.s, self.nc
        mixer = l % 3
        gi = l * 3 + 1
        SCALE = DH ** -0.5
        ps, psb = self.ps, self.psb
        with contextlib.ExitStack() as ph:
            def sb(name, shape, dt):
                return ph.enter_context(nc.sbuf_tensor(self.nm() + name, shape, dt))
            allb = []

            def mkb(name):
                b = Buf(name)
                allb.append(b)
                return b
            hT = sb("hT_a", [128, 16, S], BF16)
            hTb2 = [[mkb(f"hTa{c}_{t}") for c in range(NCH)] for t in range(2)]
            hTb = hTb2[0] + hTb2[1]
            qT = sb("qT", [128, S], BF16)
            kT = sb("kT", [128, S], BF16)
            vv = sb("vv", [128, 16, 128], BF16)
            qb, kb_, vb = mkb("qT"), mkb("kT"), mkb("vv")
            tmp = [sb(f"tmp{i}", [128, 512], F32) for i in range(6)]
            tmpb = [mkb(f"tmp{i}") for i in range(6)]
            et = [sb(f"et{i}", [128, 512], BF16) for i in range(4)]
            etb = [mkb(f"et{i}") for i in range(4)]
            rden = sb("rden", [128, 512], F32)
            rdb = mkb("rden")
            osb = sb("osb", [128, 512], F32)
            osbb = mkb("osb")
            mst = [sb(f"mst{i}", [128, 512], BF16) for i in range(2)]
            mstb = [mkb("mst0"), mkb("mst1")]
            mst_sem = [s.dma_sem(f"mst_sem{l}_{i}") for i in range(2)]
            aux_sem = s.dma_sem(f"aux_sem{l}")
            mload_sem = s.dma_sem(f"mload_sem{l}")
            mTsb = [mkb(f"mTs{q}") for q in range(4)]
            auxb = mkb("aux")

            for tt in range(2):
                self.emit_norm(tt * 1024, 1024, gi, lambda c: hT[:, c, tt * 1024:(tt + 1) * 1024], hTb2[tt])

            if mixer == 0:
                wf = sb("wf", [128, 16, 16], BF16)
                bfb = sb("bfb", [128, 16, 16], F32)
                nlf = sb("nlf", [128, 256], F32)
                carry = sb("carry", [128, 16, 16], F32)
                P_all = sb("P_all", [128, 16, 16], F32)
                PT = sb("PT", [16, S], F32)
                sel = sb("sel_sb", [16, 2048], F32)
                CB = sb("CB", [128, S], F32)
                cbb = mkb("CB")
                win = self.W_in(l).rearrange("(c p) n -> p c n", p=128)
                s.dma("pool", [(wf[:], win[:, :, 3 * D:3 * D + H])], [], [auxb], aux_sem)
                bf = self.W_small(f"b_f_{l}")
                bsrc = bass.AP(tensor=bf.tensor, offset=0, ap=[[0, 128], [0, 16], [1, 16]])
                s.dma("sp", [(bfb[:], bsrc), (sel[:], self.sel_d)], [], [auxb], aux_sem)

                def fmm():
                    r = None
                    for kb in range(16):
                        for c in range(NCH):
                            r = nc.tensor.matmul(ps[2][:, kb * 16:(kb + 1) * 16],
                                                 lhsT=hT[:, c, kb * 128:(kb + 1) * 128], rhs=wf[:, c, :],
                                                 start=(c == 0), stop=(c == NCH - 1))
                    return r
                s.op("pe", fmm, hTb + [auxb], [psb[2]])
                nb = mkb("nlf")
                s.op("dve", lambda: nc.vector.tensor_tensor(out=nlf[:], in0=ps[2][:, 0:256],
                                                            in1=bfb[:].rearrange("p a b -> p (a b)"),
                                                            op=ALU.add), [psb[2], auxb], [nb])
                s.op("act", lambda: nc.scalar.activation(out=nlf[:], in_=nlf[:], func=AF.Exp, scale=-1.0),
                     [nb], [nb])
                s.op("act", lambda: nc.scalar.activation(out=nlf[:], in_=nlf[:], func=AF.Ln, bias=1.0),
                     [nb], [nb])
                s.op("pe", lambda: nc.tensor.matmul(ps[2][:, 0:256], lhsT=self.cmat(1), rhs=nlf[:],
                                                    start=True, stop=True), [nb, self.cb], [psb[2]])
                s.op("pe", lambda: nc.tensor.matmul(ps[3][:, 0:256], lhsT=self.ones_f[:], rhs=nlf[:],
                                                    start=True, stop=True), [nb, self.cb], [psb[3]])
                cab = mkb("carry")
                s.op("dve", lambda: nc.vector.memset(carry[:, 0, :], 0.0), [], [cab])
                for kb in range(1, 16):
                    s.op("dve", lambda: nc.vector.tensor_tensor(
                        out=carry[:, kb, :], in0=carry[:, kb - 1, :], in1=ps[3][:, (kb - 1) * 16:kb * 16],
                        op=ALU.add), [cab, psb[3]], [cab])
                pab = mkb("P_all")
                s.op("dve", lambda: nc.vector.tensor_tensor(
                    out=P_all[:].rearrange("p a b -> p (a b)"), in0=carry[:].rearrange("p a b -> p (a b)"),
                    in1=ps[2][:, 0:256], op=ALU.add), [cab, psb[2]], [pab])
                ptb = mkb("PT")
                for k4 in range(4):
                    bank = 2 + (k4 % 2)

                    def ptr():
                        r = None
                        for u in range(4):
                            kb = k4 * 4 + u
                            r = nc.tensor.transpose(out=ps[bank][0:16, u * 128:(u + 1) * 128],
                                                    in_=P_all[:, kb, :], identity=self.ident())
                        return r
                    s.op("pe", ptr, [pab, self.cb], [psb[bank]])
                    s.op("act", lambda: nc.scalar.copy(out=PT[0:16, k4 * 512:(k4 + 1) * 512],
                                                       in_=ps[bank][0:16, :]), [psb[bank]], [ptb])
            elif mixer == 1:
                Hh = sb("Hh", [128, S], F32)
                MT = sb("MT", [128, S], F32)
                hhb, mtb = mkb("Hh"), mkb("MT")
                hh_sem = s.dma_sem(f"hh_sem{l}")
                with contextlib.ExitStack() as ph2:
                    ohl = ph2.enter_context(nc.sbuf_tensor(self.nm() + "ohl", [33, NF], F32))
                    fps = ph2.enter_context(nc.sbuf_tensor(self.nm() + "fps", [16, NF], F32))
                    rbT = ph2.enter_context(nc.sbuf_tensor(self.nm() + "rbT", [33, 16], F32))
                    fpb = Buf("fps")
                    s.op("dve", lambda: nc.vector.memset(rbT[:], 1.0), [], [auxb])
                    s.dma("sp", [(ohl[:], self.ohlm_d),
                                 (rbT[0:32, :], self.W_small("rel_bias").rearrange("h b -> b h"))],
                          [], [auxb], aux_sem, allow_slow_non_contiguous=True)
                    for i5 in range(5):
                        c0 = i5 * 512
                        n = min(512, NF - c0)
                        bank = 2 + (i5 % 2)
                        s.op("pe", lambda: nc.tensor.matmul(ps[bank][0:16, 0:n], lhsT=rbT[0:33, :],
                                                            rhs=ohl[0:33, c0:c0 + n], start=True, stop=True),
                             [auxb], [psb[bank]])
                        s.op("act", lambda: nc.scalar.copy(out=fps[0:16, c0:c0 + n], in_=ps[bank][0:16, 0:n]),
                             [psb[bank]], [fpb])
                    s.dma("sp", [(self.fpd, fps[:])], [fpb], [self.fpdb], aux_sem)
                    s.barrier()
            else:
                spf = [sb(f"spf{i}", [128, 512], F32) for i in range(3)]
                spb = [mkb(f"spf{i}") for i in range(3)]
                spe = [sb(f"spe{i}", [128, 512], F32) for i in range(2)]
                speb = [mkb(f"spe{i}") for i in range(2)]
                sph = [sb(f"sph{i}", [128, 512], BF16) for i in range(3)]
                sphb = [mkb(f"sph{i}") for i in range(3)]
                ltb = sb("lt_bf", [128, 128], BF16)
                s.op("dve", lambda: nc.vector.tensor_copy(out=ltb[:], in_=self.cmat(2)), [self.cb], [auxb])
                R = [sb(f"R_sb{i}", [128, 512], F32) for i in range(4)]
                rb = [mkb(f"R{i}") for i in range(4)]

            kproj = 0
            mi = 0
            for h in range(H):
                slot, slb = self.acquire("wqkv")
                wq = slot[:, 0:3 * 16 * 128].rearrange("p (m c n) -> p m c n", m=3, c=16)
                for (m, dst, dbuf, scl) in ((0, qT, qb, SCALE), (1, kT, kb_, 1.0)):
                    for tq in range(4):
                        bank = 2 + (kproj % 2)
                        kproj += 1

                        def pm():
                            r = None
                            for c in range(NCH):
                                r = nc.tensor.matmul(ps[bank][:, :], lhsT=wq[:, m, c, :],
                                                     rhs=hT[:, c, tq * 512:(tq + 1) * 512],
                                                     start=(c == 0), stop=(c == NCH - 1))
                            return r
                        s.op("pe", pm, [slb] + hTb2[tq // 2], [psb[bank]])
                        s.op("act", lambda: nc.scalar.activation(out=dst[:, tq * 512:(tq + 1) * 512],
                                                                 in_=ps[bank][:, :], func=AF.Copy, scale=scl),
                             [psb[bank]], [dbuf])
                for k4 in range(4):
                    bank = 2 + (kproj % 2)
                    kproj += 1

                    def vm():
                        r = None
                        for u in range(4):
                            kb = k4 * 4 + u
                            for c in range(NCH):
                                r = nc.tensor.matmul(ps[bank][:, u * 128:(u + 1) * 128],
                                                     lhsT=hT[:, c, kb * 128:(kb + 1) * 128], rhs=wq[:, 2, c, :],
                                                     start=(c == 0), stop=(c == NCH - 1))
                        return r
                    s.op("pe", vm, [slb] + hTb2[k4 // 2], [psb[bank]])
                    s.op("dve", lambda: nc.vector.tensor_copy(
                        out=vv[:, k4 * 4:(k4 + 1) * 4, :],
                        in_=ps[bank][:, :].rearrange("p (u t) -> p u t", u=4)), [psb[bank]], [vb])
                self.prefetch()
                if mixer == 0:
                    for tq in range(4):
                        bank = 2 + (kproj % 2)
                        kproj += 1
                        s.op("pe", lambda: nc.tensor.matmul(ps[bank][:, :], lhsT=sel[0:16, h * 128:(h + 1) * 128],
                                                            rhs=PT[0:16, tq * 512:(tq + 1) * 512],
                                                            start=True, stop=True), [ptb, auxb], [psb[bank]])
                        s.op("act", lambda: nc.scalar.activation(out=CB[:, tq * 512:(tq + 1) * 512],
                                                                 in_=ps[bank][:, :], func=AF.Copy, scale=-1.0),
                             [psb[bank]], [cbb])
                elif mixer == 1:
                    hsrc = bass.AP(tensor=self.fpd.tensor, offset=h * NF, ap=[[1, 128], [1, S]])
                    s.dma("sp", [(Hh[:], hsrc)], [self.fpdb], [hhb], hh_sem)
                    for tq in range(4):
                        bank = 2 + (kproj % 2)
                        kproj += 1
                        s.op("pe", lambda: nc.tensor.matmul(ps[bank][:, :], lhsT=self.cmat(3),
                                                            rhs=Hh[:, tq * 512:(tq + 1) * 512],
                                                            start=True, stop=True), [hhb, self.cb], [psb[bank]])
                        s.op("act", lambda: nc.scalar.copy(out=MT[:, tq * 512:(tq + 1) * 512], in_=ps[bank][:, :]),
                             [psb[bank]], [mtb])

                steps = []
                for G in range(4):
                    nJ = 4 * G + 4
                    Js = list(range(nJ)) if mixer != 2 else list(range(nJ - 1, -1, -1))
                    for n, J in enumerate(Js):
                        r0 = max(0, J - 4 * G)
                        steps.append((G, n, J, r0 * 128, 512 - r0 * 128, J >= 4 * G, n == 0, n == nJ - 1))
                NS = len(steps)
                stbanks = (0, 1, 4, 5) if mixer != 2 else (2, 4, 5)

                def pe_combo(parts):
                    parts = [p for p in parts if p is not None]
                    if not parts:
                        return

                    def f():
                        r = None
                        for p in parts:
                            r = p[0]()
                        return r
                    rd, wr = [], []
                    for p in parts:
                        rd += p[1]
                        wr += p[2]
                    s.op("pe", f, rd, wr)

                def part_ST(g):
                    if not (0 <= g < NS):
                        return None
                    G, n, J, c0, ncol, diag, first, last = steps[g]
                    bank = stbanks[g % len(stbanks)]
                    return (lambda: nc.tensor.matmul(ps[bank][:, 0:ncol], lhsT=kT[:, J * 128:(J + 1) * 128],
                                                     rhs=qT[:, G * 512 + c0:(G + 1) * 512],
                                                     start=True, stop=True), [kb_, qb], [psb[bank]])

                def part_PV(g, with_den):
                    if not (0 <= g < NS):
                        return None
                    G, n, J, c0, ncol, diag, first, last = steps[g]
                    ei = g % 4

                    def f():
                        r = nc.tensor.matmul(ps[6][:, c0:512], lhsT=vv[:, J, :], rhs=et[ei][:, 0:ncol],
                                             start=first, stop=last)
                        if with_den:
                            r = nc.tensor.matmul(ps[7][:, c0:512], lhsT=self.ones_bf[:], rhs=et[ei][:, 0:ncol],
                                                 start=first, stop=last)
                        return r
                    return (f, [vb, etb[ei], self.cb], [psb[6], psb[7]] if with_den else [psb[6]])

                def store_m(G):
                    nonlocal mi
                    s.dma("sp", [(self.mTd[h * 128:(h + 1) * 128, G * 512:(G + 1) * 512], mst[mi][:])],
                          [mstb[mi]], [self.mTdb[h][G]], mst_sem[mi])
                    mi = 1 - mi

                if mixer != 2:
                    def emit_EW(g):
                        G, n, J, c0, ncol, diag, first, last = steps[g]
                        bank = stbanks[g % 4]
                        ti = g % 6
                        ei = g % 4
                        if mixer == 0:
                            s.op("dve", lambda: nc.vector.scalar_tensor_tensor(
                                out=tmp[ti][:, 0:ncol], in0=ps[bank][:, 0:ncol], scalar=P_all[:, J, h:h + 1],
                                in1=CB[:, G * 512 + c0:(G + 1) * 512], op0=ALU.add, op1=ALU.add),
                                [psb[bank], cbb, pab], [tmpb[ti]])
                            if diag:
                                s.op("dve", lambda: nc.vector.tensor_tensor(
                                    out=tmp[ti][:, 0:128], in0=tmp[ti][:, 0:128], in1=self.cmat(4),
                                    op=ALU.add), [tmpb[ti], self.cb], [tmpb[ti]])
                            s.op("act", lambda: nc.scalar.activation(
                                out=et[ei][:, 0:ncol], in_=tmp[ti][:, 0:ncol], func=AF.Exp),
                                [tmpb[ti]], [etb[ei]])
                        else:
                            mc0 = 128 * (4 * G - J) + c0
                            s.op("dve", lambda: nc.vector.tensor_tensor(
                                out=tmp[ti][:, 0:ncol], in0=ps[bank][:, 0:ncol],
                                in1=MT[:, mc0:mc0 + ncol], op=ALU.add), [psb[bank], mtb], [tmpb[ti]])
                            s.op("act", lambda: nc.scalar.activation(
                                out=et[ei][:, 0:ncol], in_=tmp[ti][:, 0:ncol], func=AF.Exp),
                                [tmpb[ti]], [etb[ei]])
                    SK = 3
                    pe_combo([part_ST(g) for g in range(min(SK, NS))])
                    for g in range(NS):
                        emit_EW(g)
                        pe_combo([part_PV(g, True), part_ST(g + SK)])
                        if steps[g][7]:
                            G = steps[g][0]
                            s.op("act", lambda: nc.scalar.activation(out=rden[:], in_=ps[7][:, :], func=AF.Ln),
                                 [psb[7]], [rdb])
                            s.op("act", lambda: nc.scalar.copy(out=osb[:], in_=ps[6][:, :]), [psb[6]], [osbb])
                            s.op("act", lambda: nc.scalar.activation(out=rden[:], in_=rden[:], func=AF.Exp,
                                                                     scale=-1.0), [rdb], [rdb])
                            s.op("pool", lambda: nc.gpsimd.tensor_tensor(out=mst[mi][:], in0=osb[:], in1=rden[:],
                                                                         op=ALU.mult), [osbb, rdb], [mstb[mi]])
                            store_m(G)
                else:
                    def Rof(G, n):
                        return 2 * (G % 2) + (n % 2)

                    def S1(g):
                        G, n, J, c0, ncol, diag, first, last = steps[g]
                        bank = stbanks[g % 3]
                        si = g % 2
                        s.op("act", lambda: nc.scalar.activation(out=spe[si][:, 0:ncol], in_=ps[bank][:, 0:ncol],
                                                                 func=AF.Exp), [psb[bank]], [speb[si]])
                        s.op("act", lambda: nc.scalar.activation(out=spf[si][:, 0:ncol], in_=spe[si][:, 0:ncol],
                                                                 func=AF.Ln, bias=1.0), [speb[si]], [spb[si]])
                        hi = g % 3
                        if g % 2 == 0:
                            s.op("act", lambda: nc.scalar.activation(out=sph[hi][:, 0:ncol], in_=spe[si][:, 0:ncol],
                                                                     func=AF.Ln, bias=1.0), [speb[si]], [sphb[hi]])
                        else:
                            s.op("dve", lambda: nc.vector.tensor_copy(out=sph[hi][:, 0:ncol], in_=spf[si][:, 0:ncol]),
                                 [spb[si]], [sphb[hi]])

                    def S2(g):
                        G, n, J, c0, ncol, diag, first, last = steps[g]
                        bank = stbanks[g % 3]
                        si = g % 2
                        ti = g % 6
                        if first:
                            for rr in range(2):
                                ri = 2 * (G % 2) + rr
                                s.op("dve", lambda: nc.vector.memset(R[ri][:], 0.0), [], [rb[ri]])
                        if diag:
                            hi = g % 3
                            s.op("pool", lambda: nc.gpsimd.tensor_tensor(
                                out=sph[hi][:, 0:128], in0=spf[si][:, 0:128], in1=self.cmat(5),
                                op=ALU.mult), [spb[si], self.cb], [sphb[hi]])
                        s.op("dve", lambda: nc.vector.tensor_tensor(
                            out=tmp[ti][:, 0:ncol], in0=ps[bank][:, 0:ncol], in1=spf[si][:, 0:ncol],
                            op=ALU.subtract), [psb[bank], spb[si]], [tmpb[ti]])

                    def part_WC(g):
                        if not (0 <= g < NS):
                            return None
                        G, n, J, c0, ncol, diag, first, last = steps[g]
                        si = g % 3
                        wb = g % 2
                        cbk = (3, 7)[g % 2]

                        def f():
                            nc.tensor.matmul(ps[wb][:, 0:ncol], lhsT=ltb[:], rhs=sph[si][:, 0:ncol],
                                             start=True, stop=True)
                            return nc.tensor.matmul(ps[cbk][:, 0:ncol], lhsT=self.ones_bf[:],
                                                    rhs=sph[si][:, 0:ncol], start=True, stop=True)
                        return (f, [sphb[si], self.cb, auxb], [psb[wb], psb[cbk]])

                    def S4(g):
                        G, n, J, c0, ncol, diag, first, last = steps[g]
                        wb = g % 2
                        cbk = (3, 7)[g % 2]
                        ti = g % 6
                        rc, rn = Rof(G, n), Rof(G, n + 1)
                        s.op("dve", lambda: nc.vector.tensor_tensor(
                            out=tmp[ti][:, 0:ncol], in0=tmp[ti][:, 0:ncol], in1=ps[wb][:, 0:ncol],
                            op=ALU.subtract), [tmpb[ti], psb[wb]], [tmpb[ti]])
                        if not last:
                            s.op("dve", lambda: nc.vector.tensor_tensor(
                                out=R[rn][:, c0:512], in0=R[rc][:, c0:512], in1=ps[cbk][:, 0:ncol],
                                op=ALU.add), [rb[rc], psb[cbk]], [rb[rn]])

                    def S5(g):
                        G, n, J, c0, ncol, diag, first, last = steps[g]
                        ti = g % 6
                        rc = Rof(G, n)
                        s.op("pool", lambda: nc.gpsimd.tensor_tensor(
                            out=tmp[ti][:, 0:ncol], in0=tmp[ti][:, 0:ncol], in1=R[rc][:, c0:512],
                            op=ALU.subtract), [tmpb[ti], rb[rc]], [tmpb[ti]])
                        if diag:
                            s.op("pool", lambda: nc.gpsimd.tensor_tensor(
                                out=tmp[ti][:, 0:128], in0=tmp[ti][:, 0:128], in1=self.cmat(6),
                                op=ALU.add), [tmpb[ti], self.cb], [tmpb[ti]])

                    def S6(g):
                        G, n, J, c0, ncol, diag, first, last = steps[g]
                        ti = g % 6
                        ei = g % 4
                        s.op("act", lambda: nc.scalar.activation(
                            out=et[ei][:, 0:ncol], in_=tmp[ti][:, 0:ncol], func=AF.Exp),
                            [tmpb[ti]], [etb[ei]])
                    nonpe = {1: S1, 2: S2, 4: S4, 5: S5, 6: S6}
                    for it in range(NS + 7):
                        for kk in (6, 5, 4, 2, 1):
                            g = it - kk
                            if 0 <= g < NS:
                                nonpe[kk](g)
                        pe_combo([part_PV(it - 7, False), part_WC(it - 3), part_ST(it)])
                        g7 = it - 7
                        if 0 <= g7 < NS and steps[g7][7]:
                            s.op("act", lambda: nc.scalar.copy(out=mst[mi][:], in_=ps[6][:, :]), [psb[6]], [mstb[mi]])
                            store_m(steps[g7][0])

            mTv = self.mTd.rearrange("(c p) t -> p c t", p=128)
            for tq in range(4):
                s.dma("sp", [(hT[:, :, tq * 512:(tq + 1) * 512], mTv[:, :, tq * 512:(tq + 1) * 512])],
                      [self.mTdb[c][tq] for c in range(NCH)], [mTsb[tq]] + hTb, mload_sem)
            k = 0
            for og in range(4):
                slot, slb = self.acquire("wo")
                wo = slot[:, 0:8192].rearrange("p (c o) -> p c o", c=16)
                for tq in range(4):
                    for ocl in range(4):
                        oc = og * 4 + ocl
                        bank = 2 + (k % 2)
                        k += 1
                        xi = self.xin_load(oc, tq * 512, 512)

                        def om():
                            r = None
                            for c in range(NCH):
                                r = nc.tensor.matmul(ps[bank][:, :], lhsT=wo[:, c, ocl * 128:(ocl + 1) * 128],
                                                     rhs=hT[:, c, tq * 512:(tq + 1) * 512],
                                                     start=(c == 0), stop=(c == NCH - 1))
                            return r
                        s.op("pe", om, [slb, mTsb[tq]], [psb[bank]])
                        s.op("dve", lambda: nc.vector.tensor_tensor(
                            out=self.xin[xi][:, 0:512], in0=ps[bank][:, :], in1=self.xin[xi][:, 0:512],
                            op=ALU.add), [psb[bank], self.xinb[xi]], [self.xinb[xi]])
                        self.next_stats(xi, 0, 4 + tq, tq * 512, oc == 0, oc == NCH - 1)
                        self.xin_store(xi, oc, tq * 512, 512)
            self.flush_stats()
            self.have_stats = True
            s.barrier(allb)


def gather_inputs(prog, x_c, ws, consts):
    m = {"x": x_c}
    for name in prog.dram_in:
        if "@" in name:
            parts = name.split("@")
            a = ws[parts[0]]
            for p_ in parts[1:]:
                a = a[int(p_)]
            m[name] = np.ascontiguousarray(a)
        else:
            m[name] = ws[name]
    m.update(consts)
    return m


_CACHE = {}


def kernel(**inputs):
    consts = make_consts()
    x = np.ascontiguousarray(np.asarray(inputs["x"], dtype=np.float32))
    ws = {n: np.ascontiguousarray(np.asarray(inputs[n], dtype=np.float32)) for n in W_NAMES}
    if "p" not in _CACHE:
        _CACHE["p"] = Prog()
    prog = _CACHE["p"]
    in_maps = [gather_inputs(prog, x[c], ws, consts) for c in range(NCORES)]
    res = run_bass_kernel_spmd(prog.nc, in_maps, core_ids=list(range(NCORES)))
    return np.stack([np.asarray(r["y"]) for r in res.results], axis=0).astype(np.float32)
```

```python
import contextlib
import math
import numpy as np
import concourse.bass as bass
import concourse.mybir as mybir
from concourse.bass_utils import run_bass_kernel_spmd

F32 = mybir.dt.float32
BF16 = mybir.dt.bfloat16
AF = mybir.ActivationFunctionType
ALU = mybir.AluOpType

S = 2048
D = 2048
FF = 5632
H = 16
DH = 128
DEPTH = 4
NCH = 16
NFC = 44
EPS = 1e-6
NEG = -30000.0
NCORES = 8
NF = 2176

W_NAMES = ["norm_g", "ffn_w_gate", "ffn_w_up", "ffn_w_down", "w_in_0", "b_f_0", "w_in_1",
           "w_in_2", "w_in_3", "b_f_3", "w_out", "rel_bias", "final_g"]
W_SHAPES = {
    "norm_g": [DEPTH, 3, D], "ffn_w_gate": [DEPTH, 2, D, FF], "ffn_w_up": [DEPTH, 2, D, FF],
    "ffn_w_down": [DEPTH, 2, FF, D], "w_in_0": [D, 3 * D + H], "b_f_0": [H],
    "w_in_1": [D, 3 * D], "w_in_2": [D, 3 * D], "w_in_3": [D, 3 * D + H], "b_f_3": [H],
    "w_out": [DEPTH, D, D], "rel_bias": [H, 32], "final_g": [D],
}


def _t5_bucket_np(dist):
    dist = np.asarray(dist, dtype=np.int64)
    d = np.maximum(dist, 1).astype(np.float32)
    large = 16 + (np.log(d / np.float32(16.0)) / np.float32(math.log(2048 / 16)) * np.float32(16.0)).astype(np.int32)
    large = np.minimum(large, 31)
    return np.where(dist < 16, dist, large).astype(np.int64)


def make_consts():
    i = np.arange(128)
    c128 = np.zeros((7, 128, 128), np.float32)
    c128[0] = np.eye(128)
    c128[1] = (i[:, None] <= i[None, :])
    c128[2] = (i[:, None] > i[None, :])
    c128[3] = (i[:, None] + i[None, :] == 127)
    c128[4] = np.where(i[:, None] <= i[None, :], 0.0, NEG)
    c128[5] = (i[:, None] < i[None, :])
    c128[6] = np.where(i[:, None] < i[None, :], 0.0, NEG)
    c128 = np.ascontiguousarray(c128.transpose(1, 0, 2).reshape(128, 7 * 128))
    sel = np.zeros((16, 16 * 128), np.float32)
    for h in range(16):
        sel[h, h * 128:(h + 1) * 128] = 1.0
    ohlm = np.zeros((33, NF), np.float32)
    for idx in range(NF):
        dl = idx - 127
        if dl < 0 or dl > 2048:
            ohlm[32, idx] = NEG
            continue
        mult = (1 if dl <= 128 else 0) + (1 if (dl % 4 == 0 and dl <= 512) else 0) + \
               (1 if (dl % 16 == 0 and dl <= 2048) else 0)
        if mult == 0:
            ohlm[32, idx] = NEG
        else:
            ohlm[32, idx] = math.log(mult)
            ohlm[int(_t5_bucket_np(dl)), idx] = 1.0
    return {"c128": c128, "sel": sel, "ohlm": ohlm}


class Buf:
    __slots__ = ("name", "w", "r")

    def __init__(self, name):
        self.name = name
        self.w = None
        self.r = {}


class Sched:
    ENG = ("pe", "act", "dve", "pool", "sp")

    def __init__(self, nc, es):
        self.nc = nc
        self.es = es
        self.e = {"pe": nc.tensor, "act": nc.scalar, "dve": nc.vector, "pool": nc.gpsimd, "sp": nc.sync}
        self.sem = {k: es.enter_context(nc.semaphore("sem_" + k)) for k in self.ENG}
        self.cnt = {k: 0 for k in self.ENG}
        self.seen = {k: {} for k in self.ENG}
        self.dsems = []
        self.same_sync = True
        self.nwait = 0
        self.nops = 0

    def dma_sem(self, name):
        s = self.es.enter_context(self.nc.semaphore(name))
        box = [s, 0]
        self.dsems.append(box)
        return box

    def _wait(self, eng, sem, val):
        k = id(sem)
        if self.seen[eng].get(k, 0) < val:
            self.e[eng].wait_ge(sem, val)
            self.seen[eng][k] = val
            self.nwait += 1

    def _need(self, eng, reads, writes):
        need = {}

        def add(tok):
            s, v, owner = tok
            if owner == eng and (eng in ("pe", "sp") or not self.same_sync):
                return
            k = id(s)
            if k not in need or need[k][1] < v:
                need[k] = (s, v)

        for b in reads:
            if b.w is not None:
                add(b.w)
        for b in writes:
            if b.w is not None:
                add(b.w)
            for t in b.r.values():
                add(t)
        for s, v in need.values():
            self._wait(eng, s, v)

    def _commit(self, tok, reads, writes):
        for b in writes:
            b.w = tok
            b.r = {}
        k = id(tok[0])
        for b in reads:
            if b in writes:
                continue
            o = b.r.get(k)
            if o is None or o[1] < tok[1]:
                b.r[k] = tok

    def op(self, eng, fn, reads=(), writes=()):
        self._need(eng, reads, writes)
        inst = fn()
        self.cnt[eng] += 1
        inst.then_inc(self.sem[eng], 1)
        self.nops += 1
        self._commit((self.sem[eng], self.cnt[eng], eng), reads, writes)

    def dma(self, q, pairs, reads, writes, box, **kw):
        self._need(q, reads, writes)
        for (o, i) in pairs:
            inst = self.e[q].dma_start(out=o, in_=i, **kw)
            box[1] += 16
            inst.then_inc(box[0], 16)
        self._commit((box[0], box[1], None), reads, writes)

    def barrier(self, bufs_to_reset=()):
        for eng in self.ENG:
            for k in self.ENG:
                if self.cnt[k] > 0 and not (k == eng and eng in ("pe", "sp")):
                    self._wait(eng, self.sem[k], self.cnt[k])
            for box in self.dsems:
                if box[1] > 0:
                    self._wait(eng, box[0], box[1])
        for b in bufs_to_reset:
            b.w = None
            b.r = {}


class Prog:
    def __init__(self, layers=(0, 1, 2, 3), subs=("ffn1", "attn", "ffn2"), do_in=True, do_out=True,
                 compact=False):
        self.compact = compact
        self.layers = list(layers)
        self.subs = tuple(subs)
        self.do_in = do_in
        self.do_out = do_out
        self.nc = bass.Bass("TRN2", target_bir_lowering=False)
        self.build()

    def build(self):
        nc = self.nc
        with contextlib.ExitStack() as es:
            self.es = es
            s = self.s = Sched(nc, es)
            self.x = nc.dram_tensor("x", [S, D], F32, kind="ExternalInput").ap()
            self.dram_in = {}
            self.c128_d = nc.dram_tensor("c128", [128, 7 * 128], F32, kind="ExternalInput").ap()
            self.sel_d = nc.dram_tensor("sel", [16, 2048], F32, kind="ExternalInput").ap()
            self.ohlm_d = nc.dram_tensor("ohlm", [33, NF], F32, kind="ExternalInput").ap()
            self.y = nc.dram_tensor("y", [S, D], F32, kind="ExternalOutput").ap()
            self.xT = nc.dram_tensor("xT_scr", [D, S], F32, kind="Internal").ap()
            self.mTd = nc.dram_tensor("mT_scr", [D, S], BF16, kind="Internal").ap()
            self.fpd = nc.dram_tensor("fp_scr", [16, NF], F32, kind="Internal").ap()
            self.xTb = [[Buf(f"xT{c}_{q}") for q in range(4)] for c in range(NCH)]
            self.mTdb = [[Buf(f"mTd{c}_{q}") for q in range(4)] for c in range(NCH)]
            self.fpdb = Buf("fpd")

            def sb(name, shape, dt):
                return es.enter_context(nc.sbuf_tensor(self.nm() + name, shape, dt))

            self.ps = [es.enter_context(nc.psum_tensor(f"ps{i}", [128, 512], F32)) for i in range(8)]
            self.psb = [Buf(f"ps{i}") for i in range(8)]
            self.c128 = sb("c128_sb", [128, 7 * 128], F32)
            self.ones_bf = sb("ones_bf", [128, 128], BF16)
            self.ones_f = sb("ones_f", [128, 128], F32)
            self.gcol = sb("gcol", [128, 208], F32)
            self.eps_t = sb("eps_t", [128, 1], F32)
            self.rstd = sb("rstd", [128, S], F32)
            self.rstdq = [Buf(f"rstd{q}") for q in range(4)]
            self.have_stats = False
            self.pend_stats = None
            self.sq = [sb(f"sq{i}", [128, 1024], BF16) for i in range(2)]
            self.sqb = [Buf(f"sq{i}") for i in range(2)]
            self.sqh = [Buf(f"sqh{i}") for i in range(4)]
            self.xin = [sb(f"xin{i}", [128, 1024], F32) for i in range(3)]
            self.xinb = [Buf(f"xin{i}") for i in range(3)]
            self.xin_sem = [s.dma_sem(f"xin_sem{i}") for i in range(3)]
            self.xin_i = 0
            self.NSLOT = 2
            self.wsl = [sb(f"wsl{i}", [128, 8192], BF16) for i in range(self.NSLOT)]
            self.wslb = [Buf(f"wsl{i}") for i in range(self.NSLOT)]
            self.wsl_sem = [s.dma_sem(f"wsl_sem{i}") for i in range(self.NSLOT)]
            self.misc_sem = s.dma_sem("misc_sem")
            self.cb = Buf("consts")

            self.jobs = list(self.plan_jobs())
            self.job_issued = 0
            self.job_next = 0

            self.setup_consts()
            if self.do_in:
                self.phase_in()
            for l in self.layers:
                if "ffn1" in self.subs:
                    self.phase_ffn(l, 0)
                if "attn" in self.subs:
                    self.phase_attn(l)
                if "ffn2" in self.subs:
                    self.phase_ffn(l, 1)
            if self.do_out:
                self.phase_out()
            assert self.job_next == len(self.jobs), (self.job_next, len(self.jobs))
            s.barrier()

    def nm(self):
        self._uid = getattr(self, "_uid", 0) + 1
        return f"u{self._uid}_"

    def din(self, name, shape):
        if name not in self.dram_in:
            self.dram_in[name] = self.nc.dram_tensor(name, list(shape), F32, kind="ExternalInput").ap()
        return self.dram_in[name]

    def W_ff(self, nm, l, j):
        shp = W_SHAPES[nm]
        if self.compact:
            return self.din(f"{nm}@{l}@{j}", shp[2:])
        return self.din(nm, shp)[l, j]

    def W_out(self, l):
        if self.compact:
            return self.din(f"w_out@{l}", [D, D])
        return self.din("w_out", W_SHAPES["w_out"])[l]

    def W_in(self, l):
        return self.din(f"w_in_{l}", W_SHAPES[f"w_in_{l}"])

    def W_small(self, nm):
        return self.din(nm, W_SHAPES[nm])

    def ident(self):
        return self.c128[:, 0:128]

    def cmat(self, i):
        return self.c128[:, i * 128:(i + 1) * 128]

    def setup_consts(self):
        s, nc = self.s, self.nc
        s.dma("sp", [(self.c128[:], self.c128_d)], [], [self.cb], self.misc_sem)
        s.op("dve", lambda: nc.vector.memset(self.ones_bf[:], 1.0), [], [self.cb])
        s.op("dve", lambda: nc.vector.memset(self.ones_f[:], 1.0), [], [self.cb])
        s.op("dve", lambda: nc.vector.memset(self.eps_t[:], EPS), [], [self.cb])
        with contextlib.ExitStack() as ph:
            ga = ph.enter_context(nc.sbuf_tensor(self.nm() + "ga", [128, 128], F32))
            gb = ph.enter_context(nc.sbuf_tensor(self.nm() + "gb", [80, 128], F32))
            gbuf = Buf("gab")
            ng = self.W_small("norm_g").rearrange("l j (c p) -> (l j c) p", p=128)
            fg = self.W_small("final_g").rearrange("(c p) -> c p", p=128)
            s.dma("sp", [(ga[:], ng[0:128, :]), (gb[0:64, :], ng[128:192, :]), (gb[64:80, :], fg)],
                  [], [gbuf], self.misc_sem)
            p0 = self.ps[0]
            s.op("pe", lambda: nc.tensor.transpose(out=p0[:, 0:128], in_=ga[:], identity=self.ident()),
                 [gbuf, self.cb], [self.psb[0]])
            s.op("pe", lambda: nc.tensor.transpose(out=p0[:, 128:208], in_=gb[0:80, :],
                                                   identity=self.c128[0:80, 0:80]),
                 [gbuf, self.cb], [self.psb[0]])
            s.op("dve", lambda: nc.vector.tensor_copy(out=self.gcol[:], in_=p0[:, 0:208]),
                 [self.psb[0]], [self.cb])
            s.barrier()

    def plan_jobs(self):
        for l in self.layers:
            if "ffn1" in self.subs:
                yield from self.ffn_jobs(l, 0)
            if "attn" in self.subs:
                for h in range(H):
                    yield ("wqkv", l, h)
                for og in range(4):
                    yield ("wo", l, og)
            if "ffn2" in self.subs:
                yield from self.ffn_jobs(l, 1)

    def ffn_jobs(self, l, j):
        for tt in range(2):
            for fg in range(22):
                yield ("wgu", l, j, fg)
            for oc in range(16):
                yield ("wd", l, j, oc)

    def issue_job(self, idx):
        job = self.jobs[idx]
        si = idx % self.NSLOT
        slot = self.wsl[si]
        kind = job[0]
        pairs = []
        if kind == "wgu":
            _, l, j, fg = job
            dst = slot[:, 0:8192].rearrange("p (m c f) -> p m c f", m=2, c=16)
            for m, nm in enumerate(("ffn_w_gate", "ffn_w_up")):
                src = self.W_ff(nm, l, j).rearrange("(c p) f -> p c f", p=128)[:, :, fg * 256:(fg + 1) * 256]
                pairs.append((dst[:, m], src))
        elif kind == "wd":
            _, l, j, oc = job
            dst = slot[:, 0:NFC * 128].rearrange("p (c o) -> p c o", c=NFC)
            src = self.W_ff("ffn_w_down", l, j).rearrange("(c p) o -> p c o", p=128)[:, :, oc * 128:(oc + 1) * 128]
            pairs.append((dst, src))
        elif kind == "wqkv":
            _, l, h = job
            dst = slot[:, 0:3 * 16 * 128].rearrange("p (m c n) -> p m c n", m=3, c=16)
            win = self.W_in(l).rearrange("(c p) n -> p c n", p=128)
            for m in range(3):
                pairs.append((dst[:, m], win[:, :, m * D + h * 128: m * D + (h + 1) * 128]))
        elif kind == "wo":
            _, l, og = job
            dst = slot[:, 0:8192].rearrange("p (c o) -> p c o", c=16)
            src = self.W_out(l).rearrange("(c p) o -> p c o", p=128)[:, :, og * 512:(og + 1) * 512]
            pairs.append((dst, src))
        self.s.dma("pool", pairs, [], [self.wslb[si]], self.wsl_sem[si])

    def acquire(self, kind):
        idx = self.job_next
        assert self.jobs[idx][0] == kind, (self.jobs[idx], kind)
        while self.job_issued < min(len(self.jobs), idx + self.NSLOT):
            self.issue_job(self.job_issued)
            self.job_issued += 1
        self.job_next += 1
        si = idx % self.NSLOT
        return self.wsl[si], self.wslb[si]

    def prefetch(self):
        idx = self.job_next
        while self.job_issued < min(len(self.jobs), idx + self.NSLOT - 1):
            self.issue_job(self.job_issued)
            self.job_issued += 1

    def xT_bufs(self, c, t0, T):
        return [self.xTb[c][q] for q in range(t0 // 512, (t0 + T) // 512)]

    def xin_load(self, c, t0, T):
        i = self.xin_i
        self.xin_i = (i + 1) % 3
        self.s.dma("sp", [(self.xin[i][:, 0:T], self.xT[c * 128:(c + 1) * 128, t0:t0 + T])],
                   self.xT_bufs(c, t0, T), [self.xinb[i]], self.xin_sem[i])
        return i

    def xin_store(self, i, c, t0, T):
        self.s.dma("sp", [(self.xT[c * 128:(c + 1) * 128, t0:t0 + T], self.xin[i][:, 0:T])],
                   [self.xinb[i]], self.xT_bufs(c, t0, T), self.xin_sem[i])

    def rstd_bufs(self, t0, T):
        return [self.rstdq[q] for q in range(t0 // 512, (t0 + T) // 512)]

    def finish_stats(self, bank, t0):
        s, nc = self.s, self.nc
        sl = self.rstd[:, t0:t0 + 512]
        rb_ = self.rstdq[t0 // 512]
        s.op("act", lambda: nc.scalar.activation(out=sl, in_=self.ps[bank][:, :], func=AF.Ln,
                                                 scale=1.0 / D, bias=self.eps_t[:, 0:1]),
             [self.psb[bank], self.cb], [rb_])
        s.op("act", lambda: nc.scalar.activation(out=sl, in_=sl, func=AF.Exp, scale=-0.5), [rb_], [rb_])

    def emit_stats(self, t0, T):
        s, nc = self.s, self.nc
        nh = T // 512
        for c in range(NCH):
            xi = self.xin_load(c, t0, T)
            qi = c % 2
            s.op("act", lambda: nc.scalar.activation(out=self.sq[qi][:, 0:T], in_=self.xin[xi][:, 0:T],
                                                     func=AF.Square),
                 [self.xinb[xi]], [self.sqb[qi]])

            def mm():
                r = None
                for hf in range(nh):
                    r = nc.tensor.matmul(self.ps[hf][:, :], lhsT=self.ones_bf[:],
                                         rhs=self.sq[qi][:, hf * 512:(hf + 1) * 512],
                                         start=(c == 0), stop=(c == NCH - 1))
                return r
            s.op("pe", mm, [self.sqb[qi], self.cb], [self.psb[hf] for hf in range(nh)])
        for hf in range(nh):
            self.finish_stats(hf, t0 + hf * 512)

    def next_stats(self, xi, col0, bank, t0, first, last):
        s, nc = self.s, self.nc
        self.sq_i = (getattr(self, "sq_i", 0) + 1) % 4
        qi, half = self.sq_i // 2, self.sq_i % 2
        sqv = self.sq[qi][:, half * 512:(half + 1) * 512]
        sqbuf = self.sqh[self.sq_i]
        s.op("act", lambda: nc.scalar.activation(out=sqv, in_=self.xin[xi][:, col0:col0 + 512], func=AF.Square),
             [self.xinb[xi]], [sqbuf, self.sqb[qi]])
        self.flush_stats()
        self.pend_stats = (sqv, sqbuf, bank, t0, first, last)

    def flush_stats(self):
        s, nc = self.s, self.nc
        if self.pend_stats is None:
            return
        sqv, sqbuf, bank, t0, first, last = self.pend_stats
        self.pend_stats = None
        s.op("pe", lambda: nc.tensor.matmul(self.ps[bank][:, :], lhsT=self.ones_bf[:], rhs=sqv,
                                            start=first, stop=last), [sqbuf, self.cb], [self.psb[bank]])
        if last:
            self.finish_stats(bank, t0)

    def emit_norm_chunk(self, t0, T, gi, c, dst, dbuf):
        s, nc = self.s, self.nc
        xi = self.xin_load(c, t0, T)
        s.op("dve", lambda: nc.vector.scalar_tensor_tensor(
            out=dst, in0=self.xin[xi][:, 0:T], scalar=self.gcol[:, gi * 16 + c: gi * 16 + c + 1],
            in1=self.rstd[:, t0:t0 + T], op0=ALU.mult, op1=ALU.mult),
            [self.xinb[xi], self.cb] + self.rstd_bufs(t0, T), [dbuf])

    def emit_norm(self, t0, T, gi, dst_fn, dst_bufs):
        s, nc = self.s, self.nc
        if not self.have_stats:
            self.emit_stats(t0, T)
        for c in range(NCH):
            xi = self.xin_load(c, t0, T)
            s.op("dve", lambda: nc.vector.scalar_tensor_tensor(
                out=dst_fn(c), in0=self.xin[xi][:, 0:T], scalar=self.gcol[:, gi * 16 + c: gi * 16 + c + 1],
                in1=self.rstd[:, t0:t0 + T], op0=ALU.mult, op1=ALU.mult),
                [self.xinb[xi], self.cb] + self.rstd_bufs(t0, T), [dst_bufs[c]])

    def phase_in(self):
        s, nc = self.s, self.nc
        with contextlib.ExitStack() as ph:
            xrow = [ph.enter_context(nc.sbuf_tensor(self.nm() + f"xrow{i}", [128, 2048], F32)) for i in range(2)]
            xcol = [ph.enter_context(nc.sbuf_tensor(self.nm() + f"xcol{i}", [128, 16, 128], F32)) for i in range(2)]
            xrb = [Buf("xrow0"), Buf("xrow1")]
            xcb = [Buf("xcol0"), Buf("xcol1")]
            sem_r = [s.dma_sem("xrow_sem0"), s.dma_sem("xrow_sem1")]
            sem_c = [s.dma_sem("xcol_sem0"), s.dma_sem("xcol_sem1")]
            xTv = self.xT.rearrange("(c p) t -> p c t", p=128)
            k = 0
            for tb in range(16):
                i = tb % 2
                s.dma("sp", [(xrow[i][:], self.x[tb * 128:(tb + 1) * 128, :])], [], [xrb[i]], sem_r[i])
                for c4 in range(4):
                    bank = 2 + (k % 2)
                    k += 1

                    def tr():
                        r = None
                        for u in range(4):
                            cc = c4 * 4 + u
                            r = nc.tensor.transpose(out=self.ps[bank][:, u * 128:(u + 1) * 128],
                                                    in_=xrow[i][:, cc * 128:(cc + 1) * 128],
                                                    identity=self.ident())
                        return r
                    s.op("pe", tr, [xrb[i], self.cb], [self.psb[bank]])
                    s.op("act", lambda: nc.scalar.copy(
                        out=xcol[i][:, c4 * 4:(c4 + 1) * 4, :],
                        in_=self.ps[bank][:, :].rearrange("p (u t) -> p u t", u=4)),
                        [self.psb[bank]], [xcb[i]])
                s.dma("sp", [(xTv[:, :, tb * 128:(tb + 1) * 128], xcol[i][:])], [xcb[i]],
                      [self.xTb[c][tb // 4] for c in range(NCH)], sem_c[i])
            s.barrier(xrb + xcb)

    def phase_out(self):
        s, nc = self.s, self.nc
        with contextlib.ExitStack() as ph:
            yn = ph.enter_context(nc.sbuf_tensor(self.nm() + "yn", [128, 16, 512], F32))
            ynb = [Buf(f"yn{c}") for c in range(NCH)]
            yrow = [ph.enter_context(nc.sbuf_tensor(self.nm() + f"yrow{i}", [128, 2048], F32)) for i in range(2)]
            yrb = [Buf("yrow0"), Buf("yrow1")]
            sem_y = [s.dma_sem("yrow_sem0"), s.dma_sem("yrow_sem1")]
            k = 0
            kk = 0
            for tq in range(4):
                self.emit_norm(tq * 512, 512, 12, lambda c: yn[:, c, :], ynb)
                for tb4 in range(4):
                    i = kk % 2
                    kk += 1
                    for c4 in range(4):
                        bank = 2 + (k % 2)
                        k += 1

                        def tr():
                            r = None
                            for u in range(4):
                                cc = c4 * 4 + u
                                r = nc.tensor.transpose(out=self.ps[bank][:, u * 128:(u + 1) * 128],
                                                        in_=yn[:, cc, tb4 * 128:(tb4 + 1) * 128],
                                                        identity=self.ident())
                            return r
                        s.op("pe", tr, [ynb[c4 * 4 + u] for u in range(4)] + [self.cb], [self.psb[bank]])
                        s.op("act", lambda: nc.scalar.copy(out=yrow[i][:, c4 * 512:(c4 + 1) * 512],
                                                           in_=self.ps[bank][:, :]),
                             [self.psb[bank]], [yrb[i]])
                    r0 = tq * 512 + tb4 * 128
                    s.dma("sp", [(self.y[r0:r0 + 128, :], yrow[i][:])], [yrb[i]], [], sem_y[i])
            s.barrier(ynb + yrb)

    def phase_ffn(self, l, j):
        s, nc = self.s, self.nc
        gi = l * 3 + (0 if j == 0 else 2)
        T = 1024
        with contextlib.ExitStack() as ph:
            hT = ph.enter_context(nc.sbuf_tensor(self.nm() + "hT_f", [128, 16, T], BF16))
            hTb = [Buf(f"hTf{c}") for c in range(NCH)]
            act = ph.enter_context(nc.sbuf_tensor(self.nm() + "act_f", [128, NFC, T], BF16))
            actb = [Buf(f"act{c}") for c in range(NFC)]
            sg = [ph.enter_context(nc.sbuf_tensor(self.nm() + f"sg{i}", [128, 512], F32)) for i in range(2)]
            sgb = [Buf("sg0"), Buf("sg1")]
            if not self.have_stats:
                self.emit_stats(0, T)
                self.emit_stats(T, T)
                self.have_stats = True
            self.emit_norm(0, T, gi, lambda c: hT[:, c, :], hTb)
            for tt in range(2):
                t0 = tt * T
                k = 0
                for fg in range(22):
                    slot, slb = self.acquire("wgu")
                    wv = slot[:, 0:8192].rearrange("p (m c f) -> p m c f", m=2, c=16)
                    for fcl in range(2):
                        fc = fg * 2 + fcl
                        for hf in range(2):
                            bg = 2 + (k % 2)
                            bu = 4 + (k % 2)
                            si = k % 2
                            k += 1

                            def mm(m, bank):
                                def f():
                                    r = None
                                    for c in range(NCH):
                                        r = nc.tensor.matmul(self.ps[bank][:, :],
                                                             lhsT=wv[:, m, c, fcl * 128:(fcl + 1) * 128],
                                                             rhs=hT[:, c, hf * 512:(hf + 1) * 512],
                                                             start=(c == 0), stop=(c == NCH - 1))
                                    return r
                                return f
                            s.op("pe", mm(0, bg), [slb] + hTb, [self.psb[bg]])
                            s.op("pe", mm(1, bu), [slb] + hTb, [self.psb[bu]])
                            s.op("act", lambda: nc.scalar.activation(out=sg[si][:], in_=self.ps[bg][:, :],
                                                                     func=AF.Silu),
                                 [self.psb[bg]], [sgb[si]])
                            s.op("dve", lambda: nc.vector.tensor_tensor(
                                out=act[:, fc, hf * 512:(hf + 1) * 512], in0=sg[si][:], in1=self.ps[bu][:, :],
                                op=ALU.mult), [sgb[si], self.psb[bu]], [actb[fc]])
                k = 0
                for oc in range(NCH):
                    slot, slb = self.acquire("wd")
                    wd = slot[:, 0:NFC * 128].rearrange("p (c o) -> p c o", c=NFC)
                    xi = self.xin_load(oc, t0, T)
                    for hf in range(2):
                        by = 6 + (k % 2)
                        k += 1

                        def mmd():
                            r = None
                            for fc in range(NFC):
                                r = nc.tensor.matmul(self.ps[by][:, :], lhsT=wd[:, fc, :],
                                                     rhs=act[:, fc, hf * 512:(hf + 1) * 512],
                                                     start=(fc == 0), stop=(fc == NFC - 1))
                            return r
                        s.op("pe", mmd, [slb] + actb, [self.psb[by]])
                        xs = self.xin[xi][:, hf * 512:(hf + 1) * 512]
                        s.op("dve", lambda: nc.vector.scalar_tensor_tensor(
                            out=xs, in0=self.ps[by][:, :], scalar=0.5, in1=xs, op0=ALU.mult, op1=ALU.add),
                            [self.psb[by], self.xinb[xi]], [self.xinb[xi]])
                        self.next_stats(xi, hf * 512, hf, t0 + hf * 512, oc == 0, oc == NCH - 1)
                    self.xin_store(xi, oc, t0, T)
                    if tt == 0:
                        self.emit_norm_chunk(T, T, gi, oc, hT[:, oc, :], hTb[oc])
                self.flush_stats()
            self.have_stats = True
            s.barrier(hTb + actb + sgb)

    def phase_attn(self, l):
        s, nc = self.s, self.nc
        mixer = l % 3
        gi = l * 3 + 1
        SCALE = DH ** -0.5
        ps, psb = self.ps, self.psb
        with contextlib.ExitStack() as ph:
            def sb(name, shape, dt):
                return ph.enter_context(nc.sbuf_tensor(self.nm() + name, shape, dt))
            allb = []

            def mkb(name):
                b = Buf(name)
                allb.append(b)
                return b
            hT = sb("hT_a", [128, 16, S], BF16)
            hTb2 = [[mkb(f"hTa{c}_{t}") for c in range(NCH)] for t in range(2)]
            hTb = hTb2[0] + hTb2[1]
            qT = sb("qT", [128, S], BF16)
            kT = sb("kT", [128, S], BF16)
            vv = sb("vv", [128, 16, 128], BF16)
            qb, kb_, vb = mkb("qT"), mkb("kT"), mkb("vv")
            tmp = [sb(f"tmp{i}", [128, 512], F32) for i in range(6)]
            tmpb = [mkb(f"tmp{i}") for i in range(6)]
            et = [sb(f"et{i}", [128, 512], BF16) for i in range(4)]
            etb = [mkb(f"et{i}") for i in range(4)]
            rden = sb("rden", [128, 512], F32)
            rdb = mkb("rden")
            osb = sb("osb", [128, 512], F32)
            osbb = mkb("osb")
            mst = [sb(f"mst{i}", [128, 512], BF16) for i in range(2)]
            mstb = [mkb("mst0"), mkb("mst1")]
            mst_sem = [s.dma_sem(f"mst_sem{l}_{i}") for i in range(2)]
            aux_sem = s.dma_sem(f"aux_sem{l}")
            mload_sem = s.dma_sem(f"mload_sem{l}")
            mTsb = [mkb(f"mTs{q}") for q in range(4)]
            auxb = mkb("aux")

            for tt in range(2):
                self.emit_norm(tt * 1024, 1024, gi, lambda c: hT[:, c, tt * 1024:(tt + 1) * 1024], hTb2[tt])

            if mixer == 0:
                wf = sb("wf", [128, 16, 16], BF16)
                bfb = sb("bfb", [128, 16, 16], F32)
                nlf = sb("nlf", [128, 256], F32)
                carry = sb("carry", [128, 16, 16], F32)
                P_all = sb("P_all", [128, 16, 16], F32)
                PT = sb("PT", [16, S], F32)
                sel = sb("sel_sb", [16, 2048], F32)
                CB = sb("CB", [128, S], F32)
                cbb = mkb("CB")
                win = self.W_in(l).rearrange("(c p) n -> p c n", p=128)
                s.dma("pool", [(wf[:], win[:, :, 3 * D:3 * D + H])], [], [auxb], aux_sem)
                bf = self.W_small(f"b_f_{l}")
                bsrc = bass.AP(tensor=bf.tensor, offset=0, ap=[[0, 128], [0, 16], [1, 16]])
                s.dma("sp", [(bfb[:], bsrc), (sel[:], self.sel_d)], [], [auxb], aux_sem)

                def fmm():
                    r = None
                    for kb in range(16):
                        for c in range(NCH):
                            r = nc.tensor.matmul(ps[2][:, kb * 16:(kb + 1) * 16],
                                                 lhsT=hT[:, c, kb * 128:(kb + 1) * 128], rhs=wf[:, c, :],
                                                 start=(c == 0), stop=(c == NCH - 1))
                    return r
                s.op("pe", fmm, hTb + [auxb], [psb[2]])
                nb = mkb("nlf")
                s.op("dve", lambda: nc.vector.tensor_tensor(out=nlf[:], in0=ps[2][:, 0:256],
                                                            in1=bfb[:].rearrange("p a b -> p (a b)"),
                                                            op=ALU.add), [psb[2], auxb], [nb])
                s.op("act", lambda: nc.scalar.activation(out=nlf[:], in_=nlf[:], func=AF.Exp, scale=-1.0),
                     [nb], [nb])
                s.op("act", lambda: nc.scalar.activation(out=nlf[:], in_=nlf[:], func=AF.Ln, bias=1.0),
                     [nb], [nb])
                s.op("pe", lambda: nc.tensor.matmul(ps[2][:, 0:256], lhsT=self.cmat(1), rhs=nlf[:],
                                                    start=True, stop=True), [nb, self.cb], [psb[2]])
                s.op("pe", lambda: nc.tensor.matmul(ps[3][:, 0:256], lhsT=self.ones_f[:], rhs=nlf[:],
                                                    start=True, stop=True), [nb, self.cb], [psb[3]])
                cab = mkb("carry")
                s.op("dve", lambda: nc.vector.memset(carry[:, 0, :], 0.0), [], [cab])
                for kb in range(1, 16):
                    s.op("dve", lambda: nc.vector.tensor_tensor(
                        out=carry[:, kb, :], in0=carry[:, kb - 1, :], in1=ps[3][:, (kb - 1) * 16:kb * 16],
                        op=ALU.add), [cab, psb[3]], [cab])
                pab = mkb("P_all")
                s.op("dve", lambda: nc.vector.tensor_tensor(
                    out=P_all[:].rearrange("p a b -> p (a b)"), in0=carry[:].rearrange("p a b -> p (a b)"),
                    in1=ps[2][:, 0:256], op=ALU.add), [cab, psb[2]], [pab])
                ptb = mkb("PT")
                for k4 in range(4):
                    bank = 2 + (k4 % 2)

                    def ptr():
                        r = None
                        for u in range(4):
                            kb = k4 * 4 + u
                            r = nc.tensor.transpose(out=ps[bank][0:16, u * 128:(u + 1) * 128],
                                                    in_=P_all[:, kb, :], identity=self.ident())
                        return r
                    s.op("pe", ptr, [pab, self.cb], [psb[bank]])
                    s.op("act", lambda: nc.scalar.copy(out=PT[0:16, k4 * 512:(k4 + 1) * 512],
                                                       in_=ps[bank][0:16, :]), [psb[bank]], [ptb])
            elif mixer == 1:
                Hh = sb("Hh", [128, S], F32)
                MT = sb("MT", [128, S], F32)
                hhb, mtb = mkb("Hh"), mkb("MT")
                hh_sem = s.dma_sem(f"hh_sem{l}")
                with contextlib.ExitStack() as ph2:
                    ohl = ph2.enter_context(nc.sbuf_tensor(self.nm() + "ohl", [33, NF], F32))
                    fps = ph2.enter_context(nc.sbuf_tensor(self.nm() + "fps", [16, NF], F32))
                    rbT = ph2.enter_context(nc.sbuf_tensor(self.nm() + "rbT", [33, 16], F32))
                    fpb = Buf("fps")
                    s.op("dve", lambda: nc.vector.memset(rbT[:], 1.0), [], [auxb])
                    s.dma("sp", [(ohl[:], self.ohlm_d),
                                 (rbT[0:32, :], self.W_small("rel_bias").rearrange("h b -> b h"))],
                          [], [auxb], aux_sem, allow_slow_non_contiguous=True)
                    for i5 in range(5):
                        c0 = i5 * 512
                        n = min(512, NF - c0)
                        bank = 2 + (i5 % 2)
                        s.op("pe", lambda: nc.tensor.matmul(ps[bank][0:16, 0:n], lhsT=rbT[0:33, :],
                                                            rhs=ohl[0:33, c0:c0 + n], start=True, stop=True),
                             [auxb], [psb[bank]])
                        s.op("act", lambda: nc.scalar.copy(out=fps[0:16, c0:c0 + n], in_=ps[bank][0:16, 0:n]),
                             [psb[bank]], [fpb])
                    s.dma("sp", [(self.fpd, fps[:])], [fpb], [self.fpdb], aux_sem)
                    s.barrier()
            else:
                spf = [sb(f"spf{i}", [128, 512], F32) for i in range(3)]
                spb = [mkb(f"spf{i}") for i in range(3)]
                spe = [sb(f"spe{i}", [128, 512], F32) for i in range(2)]
                speb = [mkb(f"spe{i}") for i in range(2)]
                sph = [sb(f"sph{i}", [128, 512], BF16) for i in range(3)]
                sphb = [mkb(f"sph{i}") for i in range(3)]
                ltb = sb("lt_bf", [128, 128], BF16)
                s.op("dve", lambda: nc.vector.tensor_copy(out=ltb[:], in_=self.cmat(2)), [self.cb], [auxb])
                R = [sb(f"R_sb{i}", [128, 512], F32) for i in range(4)]
                rb = [mkb(f"R{i}") for i in range(4)]

            kproj = 0
            mi = 0
            for h in range(H):
                slot, slb = self.acquire("wqkv")
                wq = slot[:, 0:3 * 16 * 128].rearrange("p (m c n) -> p m c n", m=3, c=16)
                def emit_qk(m, tq):
                    nonlocal kproj
                    dst, dbuf, scl = ((qT, qb, SCALE), (kT, kb_, 1.0))[m]
                    bank = 2 + (kproj % 2)
                    kproj += 1

                    def pm():
                        r = None
                        for c in range(NCH):
                            r = nc.tensor.matmul(ps[bank][:, :], lhsT=wq[:, m, c, :],
                                                 rhs=hT[:, c, tq * 512:(tq + 1) * 512],
                                                 start=(c == 0), stop=(c == NCH - 1))
                        return r
                    s.op("pe", pm, [slb] + hTb2[tq // 2], [psb[bank]])
                    s.op("act", lambda: nc.scalar.activation(out=dst[:, tq * 512:(tq + 1) * 512],
                                                             in_=ps[bank][:, :], func=AF.Copy, scale=scl),
                         [psb[bank]], [dbuf])

                def emit_v(k4):
                    nonlocal kproj
                    bank = 2 + (kproj % 2)
                    kproj += 1

                    def vm():
                        r = None
                        for u in range(4):
                            kb = k4 * 4 + u
                            for c in range(NCH):
                                r = nc.tensor.matmul(ps[bank][:, u * 128:(u + 1) * 128],
                                                     lhsT=hT[:, c, kb * 128:(kb + 1) * 128], rhs=wq[:, 2, c, :],
                                                     start=(c == 0), stop=(c == NCH - 1))
                        return r
                    s.op("pe", vm, [slb] + hTb2[k4 // 2], [psb[bank]])
                    s.op("dve", lambda: nc.vector.tensor_copy(
                        out=vv[:, k4 * 4:(k4 + 1) * 4, :],
                        in_=ps[bank][:, :].rearrange("p (u t) -> p u t", u=4)), [psb[bank]], [vb])
                for half in range(2):
                    for m in range(2):
                        for tq in (2 * half, 2 * half + 1):
                            emit_qk(m, tq)
                    for k4 in (2 * half, 2 * half + 1):
                        emit_v(k4)
                self.prefetch()
                if mixer == 0:
                    for tq in range(4):
                        bank = 2 + (kproj % 2)
                        kproj += 1
                        s.op("pe", lambda: nc.tensor.matmul(ps[bank][:, :], lhsT=sel[0:16, h * 128:(h + 1) * 128],
                                                            rhs=PT[0:16, tq * 512:(tq + 1) * 512],
                                                            start=True, stop=True), [ptb, auxb], [psb[bank]])
                        s.op("act", lambda: nc.scalar.activation(out=CB[:, tq * 512:(tq + 1) * 512],
                                                                 in_=ps[bank][:, :], func=AF.Copy, scale=-1.0),
                             [psb[bank]], [cbb])
                elif mixer == 1:
                    hsrc = bass.AP(tensor=self.fpd.tensor, offset=h * NF, ap=[[1, 128], [1, S]])
                    s.dma("sp", [(Hh[:], hsrc)], [self.fpdb], [hhb], hh_sem)
                    for tq in range(4):
                        bank = 2 + (kproj % 2)
                        kproj += 1
                        s.op("pe", lambda: nc.tensor.matmul(ps[bank][:, :], lhsT=self.cmat(3),
                                                            rhs=Hh[:, tq * 512:(tq + 1) * 512],
                                                            start=True, stop=True), [hhb, self.cb], [psb[bank]])
                        s.op("act", lambda: nc.scalar.copy(out=MT[:, tq * 512:(tq + 1) * 512], in_=ps[bank][:, :]),
                             [psb[bank]], [mtb])

                steps = []
                for G in range(4):
                    nJ = 4 * G + 4
                    Js = list(range(nJ)) if mixer != 2 else list(range(nJ - 1, -1, -1))
                    for n, J in enumerate(Js):
                        r0 = max(0, J - 4 * G)
                        steps.append((G, n, J, r0 * 128, 512 - r0 * 128, J >= 4 * G, n == 0, n == nJ - 1))
                NS = len(steps)
                stbanks = (0, 1, 4, 5) if mixer != 2 else (2, 4, 5)

                def pe_combo(parts):
                    parts = [p for p in parts if p is not None]
                    if not parts:
                        return

                    def f():
                        r = None
                        for p in parts:
                            r = p[0]()
                        return r
                    rd, wr = [], []
                    for p in parts:
                        rd += p[1]
                        wr += p[2]
                    s.op("pe", f, rd, wr)

                def part_ST(g):
                    if not (0 <= g < NS):
                        return None
                    G, n, J, c0, ncol, diag, first, last = steps[g]
                    bank = stbanks[g % len(stbanks)]
                    return (lambda: nc.tensor.matmul(ps[bank][:, 0:ncol], lhsT=kT[:, J * 128:(J + 1) * 128],
                                                     rhs=qT[:, G * 512 + c0:(G + 1) * 512],
                                                     start=True, stop=True), [kb_, qb], [psb[bank]])

                def part_PV(g, with_den):
                    if not (0 <= g < NS):
                        return None
                    G, n, J, c0, ncol, diag, first, last = steps[g]
                    ei = g % 4

                    def f():
                        r = nc.tensor.matmul(ps[6][:, c0:512], lhsT=vv[:, J, :], rhs=et[ei][:, 0:ncol],
                                             start=first, stop=last)
                        if with_den:
                            r = nc.tensor.matmul(ps[7][:, c0:512], lhsT=self.ones_bf[:], rhs=et[ei][:, 0:ncol],
                                                 start=first, stop=last)
                        return r
                    return (f, [vb, etb[ei], self.cb], [psb[6], psb[7]] if with_den else [psb[6]])

                def store_m(G):
                    nonlocal mi
                    s.dma("sp", [(self.mTd[h * 128:(h + 1) * 128, G * 512:(G + 1) * 512], mst[mi][:])],
                          [mstb[mi]], [self.mTdb[h][G]], mst_sem[mi])
                    mi = 1 - mi

                if mixer != 2:
                    def emit_EW(g):
                        G, n, J, c0, ncol, diag, first, last = steps[g]
                        bank = stbanks[g % 4]
                        ti = g % 6
                        ei = g % 4
                        if mixer == 0:
                            s.op("dve", lambda: nc.vector.scalar_tensor_tensor(
                                out=tmp[ti][:, 0:ncol], in0=ps[bank][:, 0:ncol], scalar=P_all[:, J, h:h + 1],
                                in1=CB[:, G * 512 + c0:(G + 1) * 512], op0=ALU.add, op1=ALU.add),
                                [psb[bank], cbb, pab], [tmpb[ti]])
                            if diag:
                                s.op("dve", lambda: nc.vector.tensor_tensor(
                                    out=tmp[ti][:, 0:128], in0=tmp[ti][:, 0:128], in1=self.cmat(4),
                                    op=ALU.add), [tmpb[ti], self.cb], [tmpb[ti]])
                            s.op("act", lambda: nc.scalar.activation(
                                out=et[ei][:, 0:ncol], in_=tmp[ti][:, 0:ncol], func=AF.Exp),
                                [tmpb[ti]], [etb[ei]])
                        else:
                            mc0 = 128 * (4 * G - J) + c0
                            s.op("dve", lambda: nc.vector.tensor_tensor(
                                out=tmp[ti][:, 0:ncol], in0=ps[bank][:, 0:ncol],
                                in1=MT[:, mc0:mc0 + ncol], op=ALU.add), [psb[bank], mtb], [tmpb[ti]])
                            s.op("act", lambda: nc.scalar.activation(
                                out=et[ei][:, 0:ncol], in_=tmp[ti][:, 0:ncol], func=AF.Exp),
                                [tmpb[ti]], [etb[ei]])
                    SK = 3
                    pe_combo([part_ST(g) for g in range(min(SK, NS))])
                    for g in range(NS):
                        emit_EW(g)
                        pe_combo([part_PV(g, True), part_ST(g + SK)])
                        if steps[g][7]:
                            G = steps[g][0]
                            s.op("act", lambda: nc.scalar.activation(out=rden[:], in_=ps[7][:, :], func=AF.Ln),
                                 [psb[7]], [rdb])
                            s.op("act", lambda: nc.scalar.copy(out=osb[:], in_=ps[6][:, :]), [psb[6]], [osbb])
                            s.op("act", lambda: nc.scalar.activation(out=rden[:], in_=rden[:], func=AF.Exp,
                                                                     scale=-1.0), [rdb], [rdb])
                            s.op("pool", lambda: nc.gpsimd.tensor_tensor(out=mst[mi][:], in0=osb[:], in1=rden[:],
                                                                         op=ALU.mult), [osbb, rdb], [mstb[mi]])
                            store_m(G)
                else:
                    def Rof(G, n):
                        return 2 * (G % 2) + (n % 2)

                    def S1(g):
                        G, n, J, c0, ncol, diag, first, last = steps[g]
                        bank = stbanks[g % 3]
                        si = g % 2
                        s.op("act", lambda: nc.scalar.activation(out=spe[si][:, 0:ncol], in_=ps[bank][:, 0:ncol],
                                                                 func=AF.Exp), [psb[bank]], [speb[si]])
                        s.op("act", lambda: nc.scalar.activation(out=spf[si][:, 0:ncol], in_=spe[si][:, 0:ncol],
                                                                 func=AF.Ln, bias=1.0), [speb[si]], [spb[si]])
                        hi = g % 3
                        if g % 2 == 0:
                            s.op("act", lambda: nc.scalar.activation(out=sph[hi][:, 0:ncol], in_=spe[si][:, 0:ncol],
                                                                     func=AF.Ln, bias=1.0), [speb[si]], [sphb[hi]])
                        else:
                            s.op("dve", lambda: nc.vector.tensor_copy(out=sph[hi][:, 0:ncol], in_=spf[si][:, 0:ncol]),
                                 [spb[si]], [sphb[hi]])

                    def S2(g):
                        G, n, J, c0, ncol, diag, first, last = steps[g]
                        bank = stbanks[g % 3]
                        si = g % 2
                        ti = g % 6
                        if first:
                            for rr in range(2):
                                ri = 2 * (G % 2) + rr
                                s.op("dve", lambda: nc.vector.memset(R[ri][:], 0.0), [], [rb[ri]])
                        if diag:
                            hi = g % 3
                            s.op("pool", lambda: nc.gpsimd.tensor_tensor(
                                out=sph[hi][:, 0:128], in0=spf[si][:, 0:128], in1=self.cmat(5),
                                op=ALU.mult), [spb[si], self.cb], [sphb[hi]])
                        s.op("dve", lambda: nc.vector.tensor_tensor(
                            out=tmp[ti][:, 0:ncol], in0=ps[bank][:, 0:ncol], in1=spf[si][:, 0:ncol],
                            op=ALU.subtract), [psb[bank], spb[si]], [tmpb[ti]])

                    def part_WC(g):
                        if not (0 <= g < NS):
                            return None
                        G, n, J, c0, ncol, diag, first, last = steps[g]
                        si = g % 3
                        wb = g % 2
                        cbk = (3, 7)[g % 2]

                        def f():
                            nc.tensor.matmul(ps[wb][:, 0:ncol], lhsT=ltb[:], rhs=sph[si][:, 0:ncol],
                                             start=True, stop=True)
                            return nc.tensor.matmul(ps[cbk][:, 0:ncol], lhsT=self.ones_bf[:],
                                                    rhs=sph[si][:, 0:ncol], start=True, stop=True)
                        return (f, [sphb[si], self.cb, auxb], [psb[wb], psb[cbk]])

                    def S4(g):
                        G, n, J, c0, ncol, diag, first, last = steps[g]
                        wb = g % 2
                        cbk = (3, 7)[g % 2]
                        ti = g % 6
                        rc, rn = Rof(G, n), Rof(G, n + 1)
                        s.op("dve", lambda: nc.vector.tensor_tensor(
                            out=tmp[ti][:, 0:ncol], in0=tmp[ti][:, 0:ncol], in1=ps[wb][:, 0:ncol],
                            op=ALU.subtract), [tmpb[ti], psb[wb]], [tmpb[ti]])
                        if not last:
                            s.op("dve", lambda: nc.vector.tensor_tensor(
                                out=R[rn][:, c0:512], in0=R[rc][:, c0:512], in1=ps[cbk][:, 0:ncol],
                                op=ALU.add), [rb[rc], psb[cbk]], [rb[rn]])

                    def S5(g):
                        G, n, J, c0, ncol, diag, first, last = steps[g]
                        ti = g % 6
                        rc = Rof(G, n)
                        s.op("pool", lambda: nc.gpsimd.tensor_tensor(
                            out=tmp[ti][:, 0:ncol], in0=tmp[ti][:, 0:ncol], in1=R[rc][:, c0:512],
                            op=ALU.subtract), [tmpb[ti], rb[rc]], [tmpb[ti]])
                        if diag:
                            s.op("pool", lambda: nc.gpsimd.tensor_tensor(
                                out=tmp[ti][:, 0:128], in0=tmp[ti][:, 0:128], in1=self.cmat(6),
                                op=ALU.add), [tmpb[ti], self.cb], [tmpb[ti]])

                    def S6(g):
                        G, n, J, c0, ncol, diag, first, last = steps[g]
                        ti = g % 6
                        ei = g % 4
                        s.op("act", lambda: nc.scalar.activation(
                            out=et[ei][:, 0:ncol], in_=tmp[ti][:, 0:ncol], func=AF.Exp),
                            [tmpb[ti]], [etb[ei]])
                    nonpe = {1: S1, 2: S2, 4: S4, 5: S5, 6: S6}
                    for it in range(NS + 7):
                        for kk in (6, 5, 4, 2, 1):
                            g = it - kk
                            if 0 <= g < NS:
                                nonpe[kk](g)
                        pe_combo([part_PV(it - 7, False), part_WC(it - 3), part_ST(it)])
                        g7 = it - 7
                        if 0 <= g7 < NS and steps[g7][7]:
                            s.op("act", lambda: nc.scalar.copy(out=mst[mi][:], in_=ps[6][:, :]), [psb[6]], [mstb[mi]])
                            store_m(steps[g7][0])

            mTv = self.mTd.rearrange("(c p) t -> p c t", p=128)
            for tq in range(4):
                s.dma("sp", [(hT[:, :, tq * 512:(tq + 1) * 512], mTv[:, :, tq * 512:(tq + 1) * 512])],
                      [self.mTdb[c][tq] for c in range(NCH)], [mTsb[tq]] + hTb, mload_sem)
            k = 0
            for og in range(4):
                slot, slb = self.acquire("wo")
                wo = slot[:, 0:8192].rearrange("p (c o) -> p c o", c=16)
                for tq in range(4):
                    for ocl in range(4):
                        oc = og * 4 + ocl
                        bank = 2 + (k % 2)
                        k += 1
                        xi = self.xin_load(oc, tq * 512, 512)

                        def om():
                            r = None
                            for c in range(NCH):
                                r = nc.tensor.matmul(ps[bank][:, :], lhsT=wo[:, c, ocl * 128:(ocl + 1) * 128],
                                                     rhs=hT[:, c, tq * 512:(tq + 1) * 512],
                                                     start=(c == 0), stop=(c == NCH - 1))
                            return r
                        s.op("pe", om, [slb, mTsb[tq]], [psb[bank]])
                        s.op("dve", lambda: nc.vector.tensor_tensor(
                            out=self.xin[xi][:, 0:512], in0=ps[bank][:, :], in1=self.xin[xi][:, 0:512],
                            op=ALU.add), [psb[bank], self.xinb[xi]], [self.xinb[xi]])
                        self.next_stats(xi, 0, 4 + tq, tq * 512, oc == 0, oc == NCH - 1)
                        self.xin_store(xi, oc, tq * 512, 512)
            self.flush_stats()
            self.have_stats = True
            s.barrier(allb)


def gather_inputs(prog, x_c, ws, consts):
    m = {"x": x_c}
    for name in prog.dram_in:
        if "@" in name:
            parts = name.split("@")
            a = ws[parts[0]]
            for p_ in parts[1:]:
                a = a[int(p_)]
            m[name] = np.ascontiguousarray(a)
        else:
            m[name] = ws[name]
    m.update(consts)
    return m


_CACHE = {}


def kernel(**inputs):
    consts = make_consts()
    x = np.ascontiguousarray(np.asarray(inputs["x"], dtype=np.float32))
    ws = {n: np.ascontiguousarray(np.asarray(inputs[n], dtype=np.float32)) for n in W_NAMES}
    if "p" not in _CACHE:
        _CACHE["p"] = Prog()
    prog = _CACHE["p"]
    in_maps = [gather_inputs(prog, x[c], ws, consts) for c in range(NCORES)]
    res = run_bass_kernel_spmd(prog.nc, in_maps, core_ids=list(range(NCORES)))
    return np.stack([np.asarray(r["y"]) for r in res.results], axis=0).astype(np.float32)
```
